# Optimizing a Trainium2 kernel written in Bass

```python
import math
import jax
import jax.numpy as jnp
from jax import lax
import numpy as np

D_MODEL = 1024
BATCH = 8
SEQ = 2048
DEPTH = 2

PLE_DIM = 256
N_EVEN = (DEPTH + 1) // 2
N_ODD = DEPTH // 2
SSD_HEADS = 16
SSD_HEAD_DIM = 64
SSD_INNER = SSD_HEADS * SSD_HEAD_DIM
SSD_GROUPS = 2
SSD_HPG = SSD_HEADS // SSD_GROUPS
SSD_STATE = 128
SSD_CONV = 4
SSD_CHUNK = 128
SSD_XBC = SSD_INNER + 2 * SSD_GROUPS * SSD_STATE
SC_DIM = 1024
SC_CONV = 3
MIX_IN = SSD_INNER + SSD_XBC + SSD_HEADS + 3 * SC_DIM
MIX_OUT = SSD_INNER + SC_DIM
ATTN_HEADS = 16
ATTN_KV_HEADS = 4
ATTN_GROUP = ATTN_HEADS // ATTN_KV_HEADS
ATTN_HEAD_DIM = 64
WINDOW = 128
ATTN_BLOCK = 128
QKV_DIM = (ATTN_HEADS + 2 * ATTN_KV_HEADS) * ATTN_HEAD_DIM
D_FF = ((8 * D_MODEL + 3 * 256 - 1) // (3 * 256)) * 256
ALPHA = (2 * DEPTH) ** 0.25
BETA = (8 * DEPTH) ** -0.25
LN_EPS = 1e-5
RMS_EPS = 1e-5

kernel_name = 'hybrid_ssd_shortconv_swa_deepnorm'


def layer_norm(x, g, b):
    xf = x.astype(jnp.float32)
    mu = jnp.mean(xf, axis=-1, keepdims=True)
    xc = xf - mu
    var = jnp.mean(xc * xc, axis=-1, keepdims=True)
    return (xc * lax.rsqrt(var + LN_EPS) * g.astype(jnp.float32) + b.astype(jnp.float32)).astype(x.dtype)


def causal_dwconv(x, w):
    k, c = w.shape
    return lax.conv_general_dilated(
        x, w[:, None, :].astype(x.dtype), window_strides=(1,), padding=[(k - 1, 0)],
        dimension_numbers=('NWC', 'WIO', 'NWC'), feature_group_count=c)


def ssd_mixer(z, xbc, dt_raw, conv_w, conv_b, dt_bias, a_log, d_skip, norm_w):
    f32 = jnp.float32
    bsz, seqlen, _ = z.shape
    nc = seqlen // SSD_CHUNK
    xbc = jax.nn.silu(causal_dwconv(xbc, conv_w) + conv_b.astype(xbc.dtype)).astype(f32)
    xs = xbc[..., :SSD_INNER].reshape(bsz, nc, SSD_CHUNK, SSD_GROUPS, SSD_HPG, SSD_HEAD_DIM)
    bm = xbc[..., SSD_INNER:SSD_INNER + SSD_GROUPS * SSD_STATE].reshape(bsz, nc, SSD_CHUNK, SSD_GROUPS, SSD_STATE)
    cm = xbc[..., SSD_INNER + SSD_GROUPS * SSD_STATE:].reshape(bsz, nc, SSD_CHUNK, SSD_GROUPS, SSD_STATE)
    dt = jax.nn.softplus(dt_raw.astype(f32) + dt_bias.astype(f32))
    a = -jnp.exp(a_log.astype(f32))
    dt = dt.reshape(bsz, nc, SSD_CHUNK, SSD_GROUPS, SSD_HPG)
    da = dt * a.reshape(SSD_GROUPS, SSD_HPG)
    xdt = xs * dt[..., None]
    a_cs = jnp.cumsum(da, axis=2)
    seg = a_cs[:, :, :, None] - a_cs[:, :, None, :]
    causal = jnp.tril(jnp.ones((SSD_CHUNK, SSD_CHUNK), dtype=bool))[:, :, None, None]
    decay_ls = jnp.exp(jnp.where(causal, seg, -jnp.inf))
    cb = jnp.einsum('bclgn,bcsgn->bclsg', cm, bm)
    y_diag = jnp.einsum('bclsgr,bcsgrp->bclgrp', cb[..., None] * decay_ls, xdt)
    decay_to_end = jnp.exp(a_cs[:, :, -1:] - a_cs)
    chunk_states = jnp.einsum('bclgn,bclgrp->bcgrpn', bm, xdt * decay_to_end[..., None])
    chunk_decay = jnp.exp(a_cs[:, :, -1])

    def step(h, inp):
        s_c, d_c = inp
        return h * d_c[..., None, None] + s_c, h

    h0 = jnp.zeros((bsz, SSD_GROUPS, SSD_HPG, SSD_HEAD_DIM, SSD_STATE), f32)
    _, h_in = lax.scan(step, h0, (jnp.moveaxis(chunk_states, 1, 0), jnp.moveaxis(chunk_decay, 1, 0)))
    h_in = jnp.moveaxis(h_in, 0, 1)
    y_off = jnp.einsum('bclgn,bcgrpn->bclgrp', cm, h_in) * jnp.exp(a_cs)[..., None]
    y = y_diag + y_off + xs * d_skip.astype(f32).reshape(SSD_GROUPS, SSD_HPG)[:, :, None]
    y = y.reshape(bsz, seqlen, SSD_INNER) * jax.nn.silu(z.astype(f32))
    yg = y.reshape(bsz, seqlen, SSD_GROUPS, SSD_INNER // SSD_GROUPS)
    yg = yg * lax.rsqrt(jnp.mean(yg * yg, axis=-1, keepdims=True) + RMS_EPS)
    return (yg.reshape(bsz, seqlen, SSD_INNER) * norm_w.astype(f32)).astype(z.dtype)


def ssd_shortconv_layer_mixer(x, w_in, conv_w, conv_b, dt_bias, a_log, d_skip, norm_w, sc_conv_w, w_out):
    proj = x @ w_in
    o1 = SSD_INNER
    o2 = o1 + SSD_XBC
    o3 = o2 + SSD_HEADS
    o4 = o3 + SC_DIM
    o5 = o4 + SC_DIM
    z, xbc, dt_raw = proj[..., :o1], proj[..., o1:o2], proj[..., o2:o3]
    sc_b, sc_c, sc_h = proj[..., o3:o4], proj[..., o4:o5], proj[..., o5:]
    y_ssd = ssd_mixer(z, xbc, dt_raw, conv_w, conv_b, dt_bias, a_log, d_skip, norm_w)
    y_sc = sc_b * causal_dwconv(sc_c * sc_h, sc_conv_w)
    return jnp.concatenate([y_ssd.astype(x.dtype), y_sc.astype(x.dtype)], axis=-1) @ w_out


def sliding_window_attention(x, w_qkv, b_qkv, sinks, w_o, b_o):
    bsz, seqlen, _ = x.shape
    nb = seqlen // ATTN_BLOCK
    qkv = x @ w_qkv + b_qkv
    nq = ATTN_HEADS * ATTN_HEAD_DIM
    nkv = ATTN_KV_HEADS * ATTN_HEAD_DIM
    q = qkv[..., :nq].reshape(bsz, nb, ATTN_BLOCK, ATTN_KV_HEADS, ATTN_GROUP, ATTN_HEAD_DIM)
    k = qkv[..., nq:nq + nkv].reshape(bsz, nb, ATTN_BLOCK, ATTN_KV_HEADS, ATTN_HEAD_DIM)
    v = qkv[..., nq + nkv:].reshape(bsz, nb, ATTN_BLOCK, ATTN_KV_HEADS, ATTN_HEAD_DIM)

    def with_prev(t):
        prev = jnp.concatenate([jnp.zeros_like(t[:, :1]), t[:, :-1]], axis=1)
        return jnp.concatenate([prev, t], axis=2)

    kk = with_prev(k)
    vv = with_prev(v)
    scale = ATTN_HEAD_DIM ** -0.5
    s = jnp.einsum('bnqkgd,bnskd->bnkgqs', q, kk, preferred_element_type=jnp.float32) * scale
    n_idx = jnp.arange(nb)[:, None, None]
    q_pos = n_idx * ATTN_BLOCK + jnp.arange(ATTN_BLOCK)[None, :, None]
    k_pos = (n_idx - 1) * ATTN_BLOCK + jnp.arange(2 * ATTN_BLOCK)[None, None, :]
    rel = q_pos - k_pos
    band = (rel >= 0) & (rel < WINDOW) & (k_pos >= 0)
    s = jnp.where(band[None, :, None, None], s, -jnp.inf)
    sink = sinks.astype(jnp.float32).reshape(ATTN_KV_HEADS, ATTN_GROUP)[None, None, :, :, None, None]
    sink = jnp.broadcast_to(sink, s.shape[:-1] + (1,))
    probs = jax.nn.softmax(jnp.concatenate([s, sink], axis=-1), axis=-1)[..., :-1]
    o = jnp.einsum('bnkgqs,bnskd->bnqkgd', probs.astype(vv.dtype), vv)
    return o.reshape(bsz, seqlen, nq) @ w_o + b_o


def swiglu(x, w_gate, w_up, w_down):
    return (jax.nn.silu(x @ w_gate) * (x @ w_up)) @ w_down


def setup_inputs(seed: int = 0) -> dict:
    key = jax.random.key(seed)
    ks = jax.random.split(key, 32)
    f32 = jnp.float32

    def nrm(k, shape, scale):
        return jax.random.normal(k, shape, f32) * scale

    x = nrm(ks[0], (BATCH, SEQ, D_MODEL), 1.0)
    p = nrm(ks[1], (DEPTH, BATCH, SEQ, PLE_DIM), 1.0)
    w_in_mix = nrm(ks[2], (N_EVEN, D_MODEL, MIX_IN), D_MODEL ** -0.5)
    ssd_conv_w = nrm(ks[3], (N_EVEN, SSD_CONV, SSD_XBC), SSD_CONV ** -0.5)
    ssd_conv_b = nrm(ks[4], (N_EVEN, SSD_XBC), 0.02)
    dt0 = jnp.exp(jax.random.uniform(ks[5], (N_EVEN, SSD_HEADS), f32, math.log(1e-3), math.log(1e-1)))
    ssd_dt_bias = dt0 + jnp.log(-jnp.expm1(-dt0))
    ssd_a_log = jnp.log(jax.random.uniform(ks[6], (N_EVEN, SSD_HEADS), f32, 1.0, 16.0))
    ssd_d = 1.0 + nrm(ks[7], (N_EVEN, SSD_HEADS), 0.1)
    ssd_norm_w = 1.0 + nrm(ks[8], (N_EVEN, SSD_INNER), 0.02)
    sc_conv_w = nrm(ks[9], (N_EVEN, SC_CONV, SC_DIM), SC_CONV ** -0.5)
    w_out_mix = nrm(ks[10], (N_EVEN, MIX_OUT, D_MODEL), BETA * MIX_OUT ** -0.5)
    w_qkv = nrm(ks[11], (N_ODD, D_MODEL, QKV_DIM), D_MODEL ** -0.5)
    b_qkv = nrm(ks[12], (N_ODD, QKV_DIM), 0.02)
    attn_sinks = nrm(ks[13], (N_ODD, ATTN_HEADS), 0.5)
    w_o = nrm(ks[14], (N_ODD, ATTN_HEADS * ATTN_HEAD_DIM, D_MODEL), BETA * (ATTN_HEADS * ATTN_HEAD_DIM) ** -0.5)
    b_o = nrm(ks[15], (N_ODD, D_MODEL), 0.02)
    ln_mix_g = 1.0 + nrm(ks[16], (DEPTH, D_MODEL), 0.02)
    ln_mix_b = nrm(ks[17], (DEPTH, D_MODEL), 0.02)
    w_ffn_gate = nrm(ks[18], (DEPTH, D_MODEL, D_FF), D_MODEL ** -0.5)
    w_ffn_up = nrm(ks[19], (DEPTH, D_MODEL, D_FF), D_MODEL ** -0.5)
    w_ffn_down = nrm(ks[20], (DEPTH, D_FF, D_MODEL), BETA * D_FF ** -0.5)
    ln_ffn_g = 1.0 + nrm(ks[21], (DEPTH, D_MODEL), 0.02)
    ln_ffn_b = nrm(ks[22], (DEPTH, D_MODEL), 0.02)
    w_ple = nrm(ks[23], (DEPTH, PLE_DIM, D_MODEL), PLE_DIM ** -0.5)
    w_ple_gate = nrm(ks[24], (DEPTH, D_MODEL, D_MODEL), D_MODEL ** -0.5)
    b_ple_gate = nrm(ks[25], (DEPTH, D_MODEL), 0.02)
    return {'x': x, 'p': p, 'w_in_mix': w_in_mix, 'ssd_conv_w': ssd_conv_w, 'ssd_conv_b': ssd_conv_b,
            'ssd_dt_bias': ssd_dt_bias, 'ssd_a_log': ssd_a_log, 'ssd_d': ssd_d, 'ssd_norm_w': ssd_norm_w,
            'sc_conv_w': sc_conv_w, 'w_out_mix': w_out_mix, 'w_qkv': w_qkv, 'b_qkv': b_qkv,
            'attn_sinks': attn_sinks, 'w_o': w_o, 'b_o': b_o, 'ln_mix_g': ln_mix_g, 'ln_mix_b': ln_mix_b,
            'w_ffn_gate': w_ffn_gate, 'w_ffn_up': w_ffn_up, 'w_ffn_down': w_ffn_down,
            'ln_ffn_g': ln_ffn_g, 'ln_ffn_b': ln_ffn_b, 'w_ple': w_ple, 'w_ple_gate': w_ple_gate,
            'b_ple_gate': b_ple_gate}


def reference(x, p, w_in_mix, ssd_conv_w, ssd_conv_b, ssd_dt_bias, ssd_a_log, ssd_d, ssd_norm_w,
              sc_conv_w, w_out_mix, w_qkv, b_qkv, attn_sinks, w_o, b_o, ln_mix_g, ln_mix_b,
              w_ffn_gate, w_ffn_up, w_ffn_down, ln_ffn_g, ln_ffn_b, w_ple, w_ple_gate, b_ple_gate):
    for i in range(DEPTH):
        j = i // 2
        if i % 2 == 0:
            mix = ssd_shortconv_layer_mixer(x, w_in_mix[j], ssd_conv_w[j], ssd_conv_b[j], ssd_dt_bias[j],
                                            ssd_a_log[j], ssd_d[j], ssd_norm_w[j], sc_conv_w[j], w_out_mix[j])
        else:
            mix = sliding_window_attention(x, w_qkv[j], b_qkv[j], attn_sinks[j], w_o[j], b_o[j])
        x = layer_norm(ALPHA * x + mix.astype(x.dtype), ln_mix_g[i], ln_mix_b[i])
        x = layer_norm(ALPHA * x + swiglu(x, w_ffn_gate[i], w_ffn_up[i], w_ffn_down[i]), ln_ffn_g[i], ln_ffn_b[i])
        gate = jax.nn.sigmoid(x @ w_ple_gate[i] + b_ple_gate[i])
        x = x + (p[i] @ w_ple[i]) * gate
    return x
```

```python
from contextlib import ExitStack
import numpy as np
import concourse.bass as bass
import concourse.mybir as mybir
from concourse.bass_utils import run_bass_kernel_spmd

F32 = mybir.dt.float32
BF16 = mybir.dt.bfloat16
AF = mybir.ActivationFunctionType
ALU = mybir.AluOpType
AX = mybir.AxisListType

ENGS = ("pe", "act", "pool", "dve", "sp")
ALPHA = 4.0 ** 0.25
LN_EPS = 1e-5
RMS_EPS = 1e-5
NEG = -30000.0
DFF = 2816
NSLOT = 5
NDMA = 6


class Sched:
    def __init__(self, nc):
        self.nc = nc
        self.ops = []
        self.eng_ops = {e: [] for e in ENGS}
        self.lastw = {}
        self.readers = {}
        self.dma_rr = {e: 0 for e in ENGS}
        self.dma_last = {}
        self.arena_seen = set()

    def _add(self, eng, fn, reads, writes, kind, dom, cost=0.3, lat=None):
        reads = list(reads)
        writes = list(writes)
        if any(k.startswith("A:") for k in reads + writes):
            reads.append("PH")
            for k in reads + writes:
                if k.startswith("A:"):
                    self.arena_seen.add(k)
        idx = len(self.ops)
        deps = set()
        for r in reads:
            w = self.lastw.get(r)
            if w is not None:
                deps.add(w)
            if r.startswith("pb") or r.startswith("ptb"):
                for rd in self.readers.get(r, ()):
                    if self.ops[rd]["eng"] != eng:
                        deps.add(rd)
        for w_ in writes:
            w = self.lastw.get(w_)
            if w is not None:
                deps.add(w)
            for rd in self.readers.get(w_, ()):
                deps.add(rd)
        self.ops.append(dict(eng=eng, fn=fn, deps=deps, kind=kind, dom=dom, idx=idx, cost=cost, lat=lat))
        self.eng_ops[eng].append(idx)
        for r in reads:
            self.readers.setdefault(r, []).append(idx)
        for w_ in writes:
            self.lastw[w_] = idx
            self.readers[w_] = []
        return idx

    def op(self, eng, fn, reads=(), writes=(), cost=0.3):
        return self._add(eng, fn, reads, writes, "c", eng, cost=cost)

    def dma(self, eng, fn, reads=(), writes=(), nbytes=65536):
        slot = self.dma_rr[eng]
        self.dma_rr[eng] = (slot + 1) % NDMA
        dom = ("dma", eng, slot)
        idx = self._add(eng, fn, reads, writes, "d", dom, cost=(1.0 if eng == "pool" else 0.2), lat=nbytes)
        prev = self.dma_last.get(dom)
        if prev is not None:
            self.ops[idx]["deps"].add(prev)
        self.dma_last[dom] = idx
        return idx

    def phase_barrier(self):
        keys = sorted(self.arena_seen)
        self.arena_seen = set()
        idx = self._add("sp", lambda g: g.nop(), [], keys + ["PH"], "c", "sp", cost=0.05)
        return idx

    def schedule(self, window=48):
        import heapq
        ops = self.ops
        n = len(ops)
        succ = [[] for _ in range(n)]
        indeg = [0] * n
        for o in ops:
            indeg[o["idx"]] = len(o["deps"])
            for d in o["deps"]:
                succ[d].append(o["idx"])
        prio = [0.0] * n
        for i in range(n - 1, -1, -1):
            m = 0.0
            for s_ in succ[i]:
                if prio[s_] > m:
                    m = prio[s_]
            prio[i] = m + ops[i]["cost"]
        ready = {e: [] for e in ENGS}
        rtime = [0.0] * n
        for i in range(n):
            if indeg[i] == 0:
                heapq.heappush(ready[ops[i]["eng"]], i)
        etime = {e: 0.0 for e in ENGS}
        dma_free = 0.0
        order = {e: [] for e in ENGS}
        done = 0
        HOP = 0.15
        while done < n:
            best = None
            for e in ENGS:
                h = ready[e]
                if not h:
                    continue
                cand = heapq.nsmallest(window, h)
                for i in cand:
                    st = max(etime[e], rtime[i])
                    key = (int(st / 0.3), -prio[i], i)
                    if best is None or key < best[0]:
                        best = (key, e, i)
            key, e, i = best
            ready[e].remove(i)
            heapq.heapify(ready[e])
            st = max(etime[e], rtime[i])
            o = ops[i]
            etime[e] = st + o["cost"]
            if o["kind"] == "d":
                fin = max(dma_free, st + 1.5) + o["lat"] / 180e3
                dma_free = fin - 1.0
            else:
                fin = st + o["cost"]
            order[e].append(i)
            done += 1
            for s_ in succ[i]:
                if fin + HOP > rtime[s_]:
                    rtime[s_] = fin + HOP
                indeg[s_] -= 1
                if indeg[s_] == 0:
                    heapq.heappush(ready[ops[s_]["eng"]], s_)
        self.eng_ops = order
        self.est_makespan = max(etime.values())

    def emit(self, sems, reorder=True):
        ops = self.ops
        if reorder:
            self.schedule()
        pos = {}
        dom_seq = {}
        for e in ENGS:
            for i in self.eng_ops[e]:
                d = ops[i]["dom"]
                seq = dom_seq.setdefault(d, [])
                pos[i] = len(seq)
                seq.append(i)
        needed = set()
        waits = {}
        for e in ENGS:
            known = {}
            for i in self.eng_ops[e]:
                op = ops[i]
                wl = {}
                for j in op["deps"]:
                    d = ops[j]["dom"]
                    if d == e and op["kind"] == "c" and e == "pe":
                        continue
                    if known.get(d, -1) >= pos[j]:
                        continue
                    if d not in wl or pos[wl[d]] < pos[j]:
                        wl[d] = j
                for d, j in wl.items():
                    known[d] = pos[j]
                    needed.add(j)
                waits[i] = wl
        count_at = {}
        cnt = {}
        for d, seq in dom_seq.items():
            c = 0
            for i in seq:
                if i in needed:
                    c += 1
                    count_at[i] = c
            cnt[d] = c
        self.stats = dict(n_ops=len(ops), n_signal=len(needed),
                          per_eng={e: len(v) for e, v in self.eng_ops.items()},
                          est_us=getattr(self, "est_makespan", None))
        engobj = {"pe": "tensor", "act": "scalar", "pool": "gpsimd", "dve": "vector", "sp": "sync"}

        def run_engine(e, eng):
            for i in self.eng_ops[e]:
                op = ops[i]
                for d, j in waits[i].items():
                    mult = 16 if isinstance(d, tuple) else 1
                    eng.wait_ge(sems[d], count_at[j] * mult)
                ins = op["fn"](eng)
                if i in needed:
                    d = op["dom"]
                    ins.then_inc(sems[d], 16 if isinstance(d, tuple) else 1)

        with self.nc.Block() as block:
            for e in ENGS:
                if not self.eng_ops[e]:
                    continue
                getattr(block, engobj[e])(lambda eng, e=e: run_engine(e, eng))


IN_SPECS = [
    ("x", [2048, 1024]), ("p", [2, 2048, 256]),
    ("w_in", [1024, 5648]), ("convw_fm", [128, 12, 4]), ("convb_fm", [128, 12]),
    ("dtb", [1, 16]), ("alog", [1, 16]), ("dsk", [1, 16]), ("normw", [1, 1024]),
    ("scw_fm", [128, 8, 3]), ("w_out", [2048, 1024]),
    ("w_qkv", [1024, 1536]), ("bq_fm", [128, 8]), ("bk_dup", [128, 4]), ("brow", [1, 2304]),
    ("sinks", [1, 16]), ("w_o", [1024, 1024]),
    ("ln_mix_g", [2, 1024]), ("ln_mix_b", [2, 1024]),
    ("w_g", [2, 1024, DFF]), ("w_u", [2, 1024, DFF]), ("w_d", [2, DFF, 1024]),
    ("ln_ffn_g", [2, 1024]), ("ln_ffn_b", [2, 1024]),
    ("w_ple", [2, 256, 1024]), ("w_pg", [2, 1024, 1024]),
    ("c_ident", [128, 128]), ("c_tri", [128, 128]), ("c_negtri", [128, 128]),
    ("c_ones", [128, 128]), ("c_maskneg", [128, 128]), ("c_amask", [128, 2, 256]),
]


def build(stop_after=None):
    nc = bass.Bass("TRN2", target_bir_lowering=False)
    D = {}
    for name, shape in IN_SPECS:
        D[name] = nc.dram_tensor(name, shape, F32, kind="ExternalInput").ap()
    out_d = nc.dram_tensor("out", [2048, 1024], F32, kind="ExternalOutput").ap()

    with ExitStack() as es:
        def sb(name, shape, dt):
            return es.enter_context(nc.sbuf_tensor("sb_" + name, shape, dt))

        def ps(name, shape, dt):
            return es.enter_context(nc.psum_tensor("ps_" + name, shape, dt))

        xres = sb("xres", [128, 8, 1024], F32)
        xT = sb("xT", [128, 8, 1024], BF16)
        slots = [sb(f"slot{i}", [128, 4224], BF16) for i in range(NSLOT)]
        ARENA_N = 23296
        arena = sb("arena", [128, ARENA_N], BF16)
        rows = [sb(f"row{i}", [128, 1024], F32) for i in range(3)]
        ft = [sb(f"ft{i}", [128, 1032], F32) for i in range(4)]
        ht = [sb(f"ht{i}", [128, 1024], BF16) for i in range(8)]
        st32 = sb("st32", [128, 2, 512], F32)
        st16 = sb("st16", [128, 2, 2, 512], BF16)
        brow = sb("brow", [128, 2304], BF16)
        ones_row = sb("ones_row", [128, 128], BF16)
        ident_f = sb("ident_f", [128, 128], F32)
        ident_b = sb("ident_b", [128, 128], BF16)
        tri_f = sb("tri_f", [128, 128], F32)
        negtri_f = sb("negtri_f", [128, 128], F32)
        ones_f = sb("ones_f", [128, 128], F32)
        maskneg_b = sb("maskneg_b", [128, 128], BF16)
        tri_b = sb("tri_b", [128, 128], BF16)
        negtri_b = sb("negtri_b", [128, 128], BF16)
        dhl = sb("dhl", [128, 2, 64], BF16)
        amask = sb("amask", [128, 2, 256], BF16)
        convw = sb("convw", [128, 12, 4], F32)
        convb = sb("convb", [128, 12], F32)
        scw = sb("scw", [128, 8, 3], F32)
        bq_fm = sb("bq_fm", [128, 8], F32)
        bk_dup = sb("bk_dup", [128, 4], F32)
        nsink = sb("nsink", [128, 16], F32)
        srow = sb("srow", [128, 4, 16], F32)
        halo_x = sb("halo_x", [128, 12, 3], F32)
        halo_s = sb("halo_s", [128, 8, 2], F32)
        kcarry = sb("kcarry", [128, 4, 128], BF16)
        vcarry = sb("vcarry", [128, 256], BF16)
        dts = sb("dts", [128, 12, 64], F32)
        stat = sb("stat", [128, 2, 6], F32)
        sm8 = sb("sm8", [128, 8, 16], F32)

        pbank = [ps(f"pb{i}", [128, 512], F32) for i in range(6)]
        ptb = [ps(f"ptb{i}", [128, 8, 128], BF16) for i in range(2)]

        sems = {}
        for e in ENGS:
            sems[e] = es.enter_context(nc.semaphore("s_" + e))
        for e in ("sp", "pool"):
            for k in range(NDMA):
                sems[("dma", e, k)] = es.enter_context(nc.semaphore(f"d_{e}_{k}"))
        S = Sched(nc)

        def nfree(ap):
            try:
                return int(ap.free_size())
            except Exception:
                return 512

        def mm(out, lhsT, rhs, start, stop, r, w):
            n_ = nfree(rhs)
            passes = 4 if lhsT.dtype == F32 else 1
            S.op("pe", lambda t: t.matmul(out, lhsT=lhsT, rhs=rhs, start=start, stop=stop), r, w,
                 cost=0.03 + passes * max(n_, 64) / 2000.0)

        def tr(out, in_, r, w):
            S.op("pe", lambda t: t.transpose(out=out, in_=in_, identity=ident_b[:]), list(r) + ["ident_b"], w, cost=0.09)

        def act(out, in_, func, r, w, **kw):
            S.op("act", lambda a: a.activation(out=out, in_=in_, func=func, **kw), r, w,
                 cost=0.22 + nfree(out) / 1050.0 + (0.12 if "accum_out" in kw else 0.0))

        def dcost(out):
            return 0.12 + nfree(out) / 900.0

        def tt(eng, out, in0, in1, op, r, w):
            S.op(eng, lambda v: v.tensor_tensor(out=out, in0=in0, in1=in1, op=op), r, w, cost=dcost(out))

        def ts(eng, out, in0, s1, s2, op0, op1, r, w):
            if s2 is None:
                S.op(eng, lambda v: v.tensor_scalar(out=out, in0=in0, scalar1=s1, scalar2=None, op0=op0), r, w, cost=dcost(out))
            else:
                S.op(eng, lambda v: v.tensor_scalar(out=out, in0=in0, scalar1=s1, scalar2=s2, op0=op0, op1=op1), r, w, cost=dcost(out))

        def stt(out, in0, scalar, in1, op0, op1, r, w):
            S.op("dve", lambda v: v.scalar_tensor_tensor(out=out, in0=in0, scalar=scalar, in1=in1, op0=op0, op1=op1), r, w, cost=dcost(out))

        def cp(eng, out, in_, r, w):
            if eng == "act":
                act(out, in_, AF.Copy, r, w)
            else:
                S.op(eng, lambda v: v.tensor_copy(out=out, in_=in_), r, w, cost=dcost(out))

        def dma(q, out, in_, r, w):
            try:
                nb = int(in_.partition_size()) * int(in_.free_size()) * 4
            except Exception:
                nb = 65536
            S.dma(q, lambda g: g.dma_start(out=out, in_=in_), r, w, nbytes=nb)

        slot_rr = [0]

        def next_slot():
            i = slot_rr[0]
            slot_rr[0] = (i + 1) % NSLOT
            return i

        def wload(dram_ap, kc, ncol):
            i = next_slot()
            view = slots[i][:, 0:kc * ncol].rearrange("p (k n) -> p k n", k=kc)
            dma("pool", view, dram_ap.rearrange("(k p) n -> p k n", p=128), [], [f"slot{i}"])
            return view, f"slot{i}"

        bank_rr = [0]

        def next_bank(n=4):
            i = bank_rr[0]
            bank_rr[0] = (i + 1) % n
            return i

        ptb_rr = [0]

        def next_ptb():
            i = ptb_rr[0]
            ptb_rr[0] = (i + 1) % 2
            return i

        xb_rr = [0]

        dma("sp", ident_f[:], D["c_ident"], [], ["ident_f"])
        dma("sp", tri_f[:], D["c_tri"], [], ["tri_f"])
        dma("sp", negtri_f[:], D["c_negtri"], [], ["negtri_f"])
        dma("sp", ones_f[:], D["c_ones"], [], ["ones_f"])
        dma("pool", amask[:], D["c_amask"], [], ["amask"])
        dma("pool", maskneg_b[:], D["c_maskneg"], [], ["maskneg_b"])
        S.op("dve", lambda v: v.memset(brow[:], 0.0), [], ["brow"])
        dma("pool", brow[0:1, :], D["brow"], [], ["brow"])
        dma("sp", convw[:], D["convw_fm"], [], ["convw"])
        dma("sp", convb[:], D["convb_fm"], [], ["convb"])
        dma("sp", scw[:], D["scw_fm"], [], ["scw"])
        dma("sp", bq_fm[:], D["bq_fm"], [], ["bq_fm"])
        dma("sp", bk_dup[:], D["bk_dup"], [], ["bk_dup"])
        for i, nm in enumerate(["dtb", "alog", "dsk", "sinks"]):
            dma("sp", srow[:, i, :], D[nm].partition_broadcast(128), [], ["srow"])
        cp("dve", ident_b[:], ident_f[:], ["ident_f"], ["ident_b"])
        cp("dve", tri_b[:], tri_f[:], ["tri_f"], ["tri_b"])
        cp("dve", negtri_b[:], negtri_f[:], ["negtri_f"], ["negtri_b"])
        S.op("dve", lambda v: v.memset(ones_row[:], 0.0), [], ["ones_row"])
        S.op("dve", lambda v: v.memset(ones_row[0:1, :], 1.0), ["ones_row"], ["ones_row"])
        S.op("dve", lambda v: v.memset(st32[:], 0.0), [], ["st32_0", "st32_1"])
        S.op("dve", lambda v: v.memset(st16[:], 0.0), [], ["st16_0_0", "st16_0_1", "st16_1_0", "st16_1_1"])
        S.op("dve", lambda v: v.memset(halo_x[:], 0.0), [], ["halo_x"])
        S.op("dve", lambda v: v.memset(halo_s[:], 0.0), [], ["halo_s"])
        S.op("dve", lambda v: v.memset(kcarry[:], 0.0), [], ["kcarry"])
        S.op("dve", lambda v: v.memset(vcarry[:], 0.0), [], ["vcarry"])
        ts("dve", nsink[:], srow[:, 3, :], -1.0, None, ALU.mult, None, ["srow"], ["nsink"])
        act(srow[:, 1, :], srow[:, 1, :], AF.Exp, ["srow"], ["srow"])
        ts("dve", srow[:, 1, :], srow[:, 1, :], -1.0, None, ALU.mult, None, ["srow"], ["srow"])

        def to_xT_a(tile):
            hb = xb_rr[0]
            xb_rr[0] = 1 - hb
            act(ht[hb][:], xres[:, tile, :], AF.Copy, [f"xres{tile}"], [f"ht{hb}"])
            return (tile, hb)

        def to_xT_b(tok):
            tile, hb = tok
            pi = next_ptb()
            for kc in range(8):
                tr(ptb[pi][:, kc, :], ht[hb][:, kc * 128:(kc + 1) * 128], [f"ht{hb}"], [f"ptb{pi}"])
            cp("dve", xT[:, :, tile * 128:(tile + 1) * 128], ptb[pi][:], [f"ptb{pi}"], [f"xT{tile}"])

        def to_xT(tile):
            to_xT_b(to_xT_a(tile))

        class XTPipe:
            def __init__(self):
                self.pend = None

            def push(self, tile):
                tok = to_xT_a(tile)
                if self.pend is not None:
                    to_xT_b(self.pend)
                self.pend = tok

            def flush(self):
                if self.pend is not None:
                    to_xT_b(self.pend)
                self.pend = None

        def ln_a(tile):
            k = f"xres{tile}"
            r = 6 + tile % 2
            q = f"_{tile % 2}"
            S.op("dve", lambda v: v.bn_stats(out=stat[:, 0, :], in_=xres[:, tile, 0:512]), [k], ["stat"])
            S.op("dve", lambda v: v.bn_stats(out=stat[:, 1, :], in_=xres[:, tile, 512:1024]), [k], ["stat"])
            S.op("dve", lambda v: v.bn_aggr(out=sm8[:, r, 0:2], in_=stat[:]), ["stat"], ["ln_mv" + q])
            ts("dve", sm8[:, r, 2:3], sm8[:, r, 1:2], LN_EPS, None, ALU.add, None, ["ln_mv" + q], ["ln_a" + q])
            act(sm8[:, r, 3:4], sm8[:, r, 2:3], AF.Sqrt, ["ln_a" + q], ["ln_b" + q])
            S.op("dve", lambda v: v.reciprocal(out=sm8[:, r, 4:5], in_=sm8[:, r, 3:4]), ["ln_b" + q], ["ln_rstd" + q])
            stt(sm8[:, r, 5:6], sm8[:, r, 0:1], -1.0, sm8[:, r, 4:5], ALU.mult, ALU.mult, ["ln_mv" + q, "ln_rstd" + q], ["ln_nmr" + q])

        def ln_b_act(tile):
            xt = xres[:, tile, :]
            k = f"xres{tile}"
            r = 6 + tile % 2
            q = f"_{tile % 2}"
            act(xt, xt, AF.Identity, [k, "ln_rstd" + q, "ln_nmr" + q], [k], scale=sm8[:, r, 4:5], bias=sm8[:, r, 5:6])

        def ln_b_dve(tile, grow, brow_):
            xt = xres[:, tile, :]
            k = f"xres{tile}"
            tt("dve", xt, xt, rows[grow][:], ALU.mult, [k, f"row{grow}"], [k])
            tt("dve", xt, xt, rows[brow_][:], ALU.add, [k, f"row{brow_}"], [k])

        def layer_norm(tile, grow, brow_):
            ln_a(tile)
            ln_b_act(tile)
            ln_b_dve(tile, grow, brow_)

        class LNPipe:
            def __init__(self):
                self.pend = None
                self.xtp = XTPipe()

            def push(self, tile):
                if self.pend is not None:
                    ln_b_act(self.pend)
                ln_a(tile)
                if self.pend is not None:
                    ln_b_dve(self.pend, 0, 1)
                    self.xtp.push(self.pend)
                self.pend = tile

            def flush(self):
                if self.pend is not None:
                    ln_b_act(self.pend)
                    ln_b_dve(self.pend, 0, 1)
                    self.xtp.push(self.pend)
                self.pend = None
                self.xtp.flush()

        def load_row(i, dram_row):
            dma("sp", rows[i][:], dram_row.partition_broadcast(128), [], [f"row{i}"])

        A_SZ, A_XS, A_BTM, A_BT, A_CT, A_YC = 0, 4096, 8192, 9216, 10240, 11264

        def mixer0(sub):
            t0 = sub * 4
            tok0 = t0 * 128
            w_in = D["w_in"]
            sz = arena[:, A_SZ:A_SZ + 4096].rearrange("p (t n) -> p t n", t=4)
            xs_tm = arena[:, A_XS:A_XS + 4096].rearrange("p (t n) -> p t n", t=4)
            b_tm = arena[:, A_BTM:A_BTM + 1024].rearrange("p (t n) -> p t n", t=4)
            bT = arena[:, A_BT:A_BT + 1024].rearrange("p (g n) -> p g n", g=2)
            cT = arena[:, A_CT:A_CT + 1024].rearrange("p (g n) -> p g n", g=2)
            ycT = arena[:, A_YC:A_YC + 8192].rearrange("p (k n) -> p k n", k=16)
            xT_r = [f"xT{t0 + i}" for i in range(4)]

            wblk = {}
            conv_bank = [0]

            def get_w(blk):
                if blk not in wblk:
                    if blk < 2:
                        wblk[blk] = wload(w_in[:, 1024 + blk * 512:1024 + (blk + 1) * 512], 8, 512)
                    else:
                        wblk[blk] = wload(w_in[:, 2048:2576], 8, 528)
                return wblk[blk]

            def conv_A(j):
                wview, wkey = get_w(j // 4)
                c0 = (j % 4) * 128
                b = conv_bank[0]
                conv_bank[0] = (b + 1) % 3
                for kc in range(8):
                    mm(pbank[b][:], wview[:, kc, c0:c0 + 128], xT[:, kc, tok0:tok0 + 512], kc == 0, kc == 7,
                       [wkey] + xT_r, [f"pb{b}"])
                fi = j % 2
                acc = ft[fi][:, 516:1028]
                cp("act", ft[fi][:, 3:515], pbank[b][:], [f"pb{b}"], [f"ft{fi}r"])
                act(acc, pbank[b][:], AF.Identity, [f"pb{b}", "convw"], [f"ft{fi}a"], scale=convw[:, j, 3:4])
                cp("dve", ft[fi][:, 0:3], halo_x[:, j, :], ["halo_x"], [f"ft{fi}r"])
                for k in range(3):
                    stt(acc, ft[fi][:, k:k + 512], convw[:, j, k:k + 1], acc, ALU.mult, ALU.add,
                        [f"ft{fi}r", f"ft{fi}a", "convw"], [f"ft{fi}a"])
                cp("dve", halo_x[:, j, :], ft[fi][:, 512:515], [f"ft{fi}r"], ["halo_x"])
                acck = f"ft{fi}a"
                if j < 8:
                    hi = 2 + (j % 2)
                    act(ht[hi][:, 0:512], acc, AF.Silu, [acck, "convb"], [f"ht{hi}"], bias=convb[:, j:j + 1])
                elif j < 10:
                    act(bT[:, j - 8, :], acc, AF.Silu, [acck, "convb"], ["A:bT"], bias=convb[:, j:j + 1])
                else:
                    act(cT[:, j - 10, :], acc, AF.Silu, [acck, "convb"], ["A:cT"], bias=convb[:, j:j + 1])

            def conv_B(j):
                if j < 8:
                    hi = 2 + (j % 2)
                    pi = next_ptb()
                    for t in range(4):
                        tr(ptb[pi][:, t, :], ht[hi][:, t * 128:(t + 1) * 128], [f"ht{hi}"], [f"ptb{pi}"])
                    cp("dve", xs_tm[:, :, j * 128:(j + 1) * 128], ptb[pi][:, 0:4, :], [f"ptb{pi}"], ["A:xs_tm"])
                elif j < 10:
                    g = j - 8
                    pi = next_ptb()
                    for t in range(4):
                        tr(ptb[pi][:, t, :], bT[:, g, t * 128:(t + 1) * 128], ["A:bT"], [f"ptb{pi}"])
                    cp("dve", b_tm[:, :, g * 128:(g + 1) * 128], ptb[pi][:, 0:4, :], [f"ptb{pi}"], ["A:b_tm"])

            wsc_cache = {}

            def sc_chunk(j):
                hblk, jj = j // 4, j % 4
                if hblk not in wsc_cache:
                    wsc_cache[hblk] = [wload(w_in[:, 2576 + si * 1024 + hblk * 512:2576 + si * 1024 + (hblk + 1) * 512], 8, 512)
                                       for si in range(3)]
                wsc = wsc_cache[hblk]
                bk = {2: 3, 1: 4, 0: 5}
                for si in (2, 1, 0):
                    b = bk[si]
                    wv, wk = wsc[si]
                    for kc in range(8):
                        mm(pbank[b][:], wv[:, kc, jj * 128:(jj + 1) * 128], xT[:, kc, tok0:tok0 + 512], kc == 0, kc == 7,
                           [wk] + xT_r, [f"pb{b}"])
                fi = 2 + (j % 2)
                hs = ft[fi][:, 516:1028]
                cp("act", hs, pbank[bk[2]][:], [f"pb{bk[2]}"], [f"ft{fi}a"])
                tt("dve", ft[fi][:, 2:514], pbank[bk[1]][:], hs, ALU.mult, [f"pb{bk[1]}", f"ft{fi}a"], [f"ft{fi}r"])
                cp("dve", ft[fi][:, 0:2], halo_s[:, j, :], ["halo_s"], [f"ft{fi}r"])
                act(hs, ft[fi][:, 2:514], AF.Identity, [f"ft{fi}r", "scw"], [f"ft{fi}a"], scale=scw[:, j, 2:3])
                for k in range(2):
                    stt(hs, ft[fi][:, k:k + 512], scw[:, j, k:k + 1], hs, ALU.mult, ALU.add, [f"ft{fi}r", f"ft{fi}a", "scw"], [f"ft{fi}a"])
                cp("dve", halo_s[:, j, :], ft[fi][:, 512:514], [f"ft{fi}r"], ["halo_s"])
                tt("dve", ycT[:, 8 + j, :], pbank[bk[0]][:], hs, ALU.mult, [f"pb{bk[0]}", f"ft{fi}a"], ["A:ycT_sc"])

            order = [8, 9, 10, 11, 0, 1, 2, 3, 4, 5, 6, 7]
            sc_next = [0]
            for n_, j in enumerate(order):
                conv_A(j)
                if n_ >= 1:
                    conv_B(order[n_ - 1])
                if n_ % 3 != 2:
                    sc_chunk(sc_next[0])
                    sc_next[0] += 1
                if j == 11:
                    wv, wk = get_w(2)
                    for t in range(4):
                        for kc in range(8):
                            mm(pbank[4][:, t * 16:(t + 1) * 16], xT[:, kc, (t0 + t) * 128:(t0 + t + 1) * 128], wv[:, kc, 512:528],
                               kc == 0, kc == 7, [wk, f"xT{t0 + t}"], ["pb4"])
                    d3 = lambda i: dts[:, i, :].rearrange("p (t h) -> p t h", t=4)
                    tt("dve", d3(0), pbank[4][:, 0:64].rearrange("p (t h) -> p t h", t=4),
                       srow[:, 0, :].unsqueeze(1).to_broadcast([128, 4, 16]), ALU.add, ["pb4", "srow"], ["dts0"])
                    stt(dts[:, 1, :], dts[:, 0, :], -1.0, dts[:, 0, :], ALU.mult, ALU.max, ["dts0"], ["dts1"])
                    act(dts[:, 2, :], dts[:, 1, :], AF.Exp, ["dts1"], ["dts2"], scale=-1.0)
                    act(dts[:, 3, :], dts[:, 2, :], AF.Ln, ["dts2"], ["dts3"], bias=1.0)
                    stt(dts[:, 4, :], dts[:, 0, :], 0.0, dts[:, 3, :], ALU.max, ALU.add, ["dts0", "dts3"], ["dts4"])
                    tt("dve", d3(5), d3(4), srow[:, 1, :].unsqueeze(1).to_broadcast([128, 4, 16]), ALU.mult, ["dts4", "srow"], ["dts5"])
                    cp("dve", dhl[:, 0, :], dts[:, 5, :], ["dts5"], ["dhl0"])
                    cp("dve", dts[:, 11, :], dhl[:, 0, :], ["dhl0"], ["dts11"])
                    tt("dve", dts[:, 11, :], dts[:, 5, :], dts[:, 11, :], ALU.subtract, ["dts5", "dts11"], ["dts11"])
                    cp("dve", dhl[:, 1, :], dts[:, 11, :], ["dts11"], ["dhl1"])
                    mm(pbank[4][:, 64:128], tri_f[:], dts[:, 5, :], True, True, ["tri_f", "dts5"], ["pb4"])
                    mm(pbank[4][:, 128:192], ones_f[:], dts[:, 5, :], True, True, ["ones_f", "dts5"], ["pb4"])
                    cp("act", dts[:, 6, :], pbank[4][:, 64:128], ["pb4"], ["dts6"])
                    act(dts[:, 7, :], pbank[4][:, 64:128], AF.Exp, ["pb4"], ["dts7"])
                    tt("dve", dts[:, 8, :], pbank[4][:, 128:192], dts[:, 6, :], ALU.subtract, ["pb4", "dts6"], ["dts8"])
                    act(dts[:, 8, :], dts[:, 8, :], AF.Exp, ["dts8"], ["dts8"])
                    act(dts[:, 9, :], pbank[4][:, 128:192], AF.Exp, ["pb4"], ["dts9"])
                    tt("dve", dts[:, 10, :], dts[:, 4, :], dts[:, 8, :], ALU.mult, ["dts4", "dts8"], ["dts10"])
            conv_B(order[-1])

            for cb in range(2):
                wv, wk = wload(w_in[:, cb * 512:(cb + 1) * 512], 8, 512)
                for t in range(4):
                    b = next_bank()
                    for kc in range(8):
                        mm(pbank[b][:], xT[:, kc, (t0 + t) * 128:(t0 + t + 1) * 128], wv[:, kc, :], kc == 0, kc == 7,
                           [wk, f"xT{t0 + t}"], [f"pb{b}"])
                    act(sz[:, t, cb * 512:(cb + 1) * 512], pbank[b][:], AF.Silu, [f"pb{b}"], ["A:sz"])

            ft2b = ft[2][:].bitcast(BF16)
            xdt_buf = [(ht[4][:], ["ht4"]), (ft2b[:, 0:1024], ["ft2r", "ft2a"])]
            xw_buf = [(ht[5][:], ["ht5"]), (ft2b[:, 1024:2048], ["ft2r", "ft2a"])]
            xsd_buf = [(ht[2][:], ["ht2"]), (ht[3][:], ["ht3"])]

            def ssd_pre(c):
                par = c % 2
                xs3 = xs_tm[:, c, :].rearrange("p (h d) -> p h d", h=16)
                for (buf, keys), row, rk in ((xdt_buf[par], 4, "dts4"), (xw_buf[par], 10, "dts10")):
                    tt("dve", buf.rearrange("p (h d) -> p h d", h=16), xs3,
                       dts[:, row, c * 16:(c + 1) * 16].unsqueeze(2).to_broadcast([128, 16, 64]), ALU.mult, ["A:xs_tm", rk], keys)
                buf, keys = xsd_buf[par]
                tt("dve", buf.rearrange("p (h d) -> p h d", h=16), xs3,
                   srow[:, 2, :].unsqueeze(2).to_broadcast([128, 16, 64]), ALU.mult, ["A:xs_tm", "srow"], keys)

            def ssd_S1(it):
                c, g, q = it // 4, (it // 2) % 2, it % 2
                csl = slice(c * 128, (c + 1) * 128)
                if q == 0:
                    r = (2 * c + g) % 4
                    mm(pbank[5][:, r * 128:(r + 1) * 128], bT[:, g, csl], cT[:, g, csl], True, True, ["A:bT", "A:cT"], ["pb5"])
                sb_ = it % 2
                k_ = f"pb{sb_}"
                col0 = c * 16 + g * 8 + q * 4
                o3 = pbank[sb_][:].rearrange("p (h l) -> p h l", h=4)
                mm(o3, ident_b[:], maskneg_b[:].unsqueeze(1).to_broadcast([128, 4, 128]), True, False, ["ident_b", "maskneg_b"], [k_])
                for v in range(2):
                    mm(o3, negtri_b[:], dhl[:, v, col0:col0 + 4].unsqueeze(2).to_broadcast([128, 4, 128]), False, False,
                       [f"dhl{v}", "negtri_b"], [k_])
                for h4 in range(4):
                    o = pbank[sb_][:, h4 * 128:(h4 + 1) * 128]
                    for v in range(2):
                        mm(o, dhl[:, v, col0 + h4:col0 + h4 + 1].to_broadcast([128, 128]), tri_b[:], False, (h4 == 3 and v == 1),
                           [f"dhl{v}", "tri_b"], [k_])

            def ssd_S2(it):
                c, g, q = it // 4, (it // 2) % 2, it % 2
                sb_ = it % 2
                r = (2 * c + g) % 4
                dsl = slice(sb_ * 512, (sb_ + 1) * 512)
                act(ht[6][:, dsl], pbank[sb_][:], AF.Exp, [f"pb{sb_}"], [f"ht6_{sb_}"])
                tt("dve", ht[7][:, dsl].rearrange("p (h l) -> p h l", h=4), ht[6][:, dsl].rearrange("p (h l) -> p h l", h=4),
                   pbank[5][:, r * 128:(r + 1) * 128].unsqueeze(1).to_broadcast([128, 4, 128]), ALU.mult,
                   [f"ht6_{sb_}", "pb5"], [f"ht7_{sb_}"])

            def ssd_S3(it):
                c, g, q = it // 4, (it // 2) % 2, it % 2
                par = c % 2
                sb_ = it % 2
                csl = slice(c * 128, (c + 1) * 128)
                xdt, xdtk = xdt_buf[par]
                if q == 0:
                    xsd, xsdk = xsd_buf[par]
                    mm(pbank[2][:], ident_b[:], xsd[:, g * 512:(g + 1) * 512], True, False, ["ident_b"] + xsdk, ["pb2"])
                for h4 in range(4):
                    h = q * 4 + h4
                    hh = g * 8 + h
                    mm(pbank[2][:, h * 64:(h + 1) * 64], ht[7][:, sb_ * 512 + h4 * 128:sb_ * 512 + (h4 + 1) * 128],
                       xdt[:, hh * 64:(hh + 1) * 64], False, (q == 1 and h4 == 3), [f"ht7_{sb_}"] + xdtk, ["pb2"])
                if q == 1:
                    xw, xwk = xw_buf[par]
                    mm(pbank[3][:], cT[:, g, csl], st16[:, par, g, :], True, True, ["A:cT", f"st16_{par}_{g}"], ["pb3"])
                    mm(pbank[4][:], b_tm[:, c, g * 128:(g + 1) * 128], xw[:, g * 512:(g + 1) * 512], True, True,
                       ["A:b_tm"] + xwk, ["pb4"])

            def ssd_S4(c, g):
                par = c % 2
                yb = ft[c % 2]
                yk = [f"ft{c % 2}r", f"ft{c % 2}a"]
                y3 = yb[:, g * 512:(g + 1) * 512].rearrange("p (h d) -> p h d", h=8)
                tt("dve", y3, pbank[3][:].rearrange("p (h d) -> p h d", h=8),
                   dts[:, 7, c * 16 + g * 8:c * 16 + g * 8 + 8].unsqueeze(2).to_broadcast([128, 8, 64]), ALU.mult,
                   ["pb3", "dts7"], yk)
                tt("dve", yb[:, g * 512:(g + 1) * 512], yb[:, g * 512:(g + 1) * 512], pbank[2][:], ALU.add, yk + ["pb2"], yk)
                s3 = st32[:, g, :].rearrange("p (h d) -> p h d", h=8)
                tt("dve", s3, s3, dts[:, 9, c * 16 + g * 8:c * 16 + g * 8 + 8].unsqueeze(2).to_broadcast([128, 8, 64]), ALU.mult,
                   [f"st32_{g}", "dts9"], [f"st32_{g}"])
                tt("dve", st32[:, g, :], st32[:, g, :], pbank[4][:], ALU.add, [f"st32_{g}", "pb4"], [f"st32_{g}"])
                cp("act", st16[:, 1 - par, g, :], st32[:, g, :], [f"st32_{g}"], [f"st16_{1 - par}_{g}"])

            def ssd_tail_elem(c):
                yb = ft[c % 2]
                yk = [f"ft{c % 2}r", f"ft{c % 2}a"]
                tt("dve", yb[:, 0:1024], yb[:, 0:1024], sz[:, c, :], ALU.mult, yk + ["A:sz"], yk)
                junk = ft[3][:, 516:1028]
                for g in range(2):
                    act(junk, yb[:, g * 512:(g + 1) * 512], AF.Square, yk, ["ft3a", f"rms_ss{g}"], accum_out=sm8[:, 1, g:g + 1])
                ts("dve", sm8[:, 1, 2:4], sm8[:, 1, 0:2], 1.0 / 512.0, RMS_EPS, ALU.mult, ALU.add, ["rms_ss0", "rms_ss1"], ["rms_a"])
                act(sm8[:, 1, 4:6], sm8[:, 1, 2:4], AF.Sqrt, ["rms_a"], ["rms_b"])
                S.op("dve", lambda v: v.reciprocal(out=sm8[:, 1, 6:8], in_=sm8[:, 1, 4:6]), ["rms_b"], ["rms_r"])
                hb = c % 2
                for g in range(2):
                    stt(ht[hb][:, g * 512:(g + 1) * 512], yb[:, g * 512:(g + 1) * 512], sm8[:, 1, 6 + g:7 + g],
                        rows[2][:, g * 512:(g + 1) * 512], ALU.mult, ALU.mult, yk + ["rms_r", "row2"], [f"ht{hb}"])

            def ssd_tail_pe(c):
                hb = c % 2
                pi = next_ptb()
                for kc in range(8):
                    tr(ptb[pi][:, kc, :], ht[hb][:, kc * 128:(kc + 1) * 128], [f"ht{hb}"], [f"ptb{pi}"])
                cp("act", ycT[:, 0:8, c * 128:(c + 1) * 128], ptb[pi][:], [f"ptb{pi}"], ["A:ycT_ssd"])

            NIT = 16
            ssd_pre(0)
            ssd_S1(0)
            pend_s4 = None
            tail_q = []
            for it in range(NIT):
                c, g, q = it // 4, (it // 2) % 2, it % 2
                if it + 1 < NIT:
                    if (it + 1) % 4 == 0:
                        ssd_pre((it + 1) // 4)
                    ssd_S1(it + 1)
                ssd_S2(it)
                if pend_s4 is not None:
                    pc, pg = pend_s4
                    ssd_S4(pc, pg)
                    if pg == 1:
                        ssd_tail_elem(pc)
                        tail_q.append((it + 1, pc))
                    pend_s4 = None
                ssd_S3(it)
                if q == 1:
                    pend_s4 = (c, g)
                while tail_q and tail_q[0][0] <= it:
                    ssd_tail_pe(tail_q.pop(0)[1])
            pc, pg = pend_s4
            ssd_S4(pc, pg)
            ssd_tail_elem(pc)
            for _, tc_ in tail_q:
                ssd_tail_pe(tc_)
            ssd_tail_pe(pc)

            lnp = LNPipe()
            w_out = D["w_out"]
            for cb in range(2):
                wviews = []
                for kh in range(2):
                    wv, wk = wload(w_out[kh * 1024:(kh + 1) * 1024, cb * 512:(cb + 1) * 512], 8, 512)
                    wviews.append((wv, wk))
                for t in range(4):
                    b = next_bank()
                    for kc in range(16):
                        wv, wk = wviews[kc // 8]
                        mm(pbank[b][:], ycT[:, kc, t * 128:(t + 1) * 128], wv[:, kc % 8, :], kc == 0, kc == 15,
                           [wk, "A:ycT_ssd", "A:ycT_sc"], [f"pb{b}"])
                    xs_ = xres[:, t0 + t, cb * 512:(cb + 1) * 512]
                    stt(xs_, xs_, ALPHA, pbank[b][:], ALU.mult, ALU.add, [f"xres{t0 + t}", f"pb{b}"], [f"xres{t0 + t}"])
                    if cb == 1:
                        lnp.push(t0 + t)
            lnp.flush()

        def ffn(layer):
            hT = arena[:, 0:22 * 1024].rearrange("p (f n) -> p f n", f=22)
            xT_r = [f"xT{t}" for t in range(8)]
            wg_d, wu_d, wd_d = D["w_g"][layer], D["w_u"][layer], D["w_d"][layer]
            for fb in range(11):
                si_ = next_slot()
                wgu = slots[si_][:, 0:4096].rearrange("p (k s n) -> p k s n", k=8, s=2)
                wkey = f"slot{si_}"
                dma("pool", wgu[:, :, 0, :], wg_d[:, fb * 256:(fb + 1) * 256].rearrange("(k p) n -> p k n", p=128), [], [wkey])
                dma("pool", wgu[:, :, 1, :], wu_d[:, fb * 256:(fb + 1) * 256].rearrange("(k p) n -> p k n", p=128), [], [wkey])
                for fc in range(2):
                    f = fb * 2 + fc
                    for nb in range(2):
                        bg = next_bank()
                        bu = next_bank()
                        for kc in range(8):
                            mm(pbank[bg][:], wgu[:, kc, 0, fc * 128:(fc + 1) * 128], xT[:, kc, nb * 512:(nb + 1) * 512], kc == 0, kc == 7,
                               [wkey] + xT_r[nb * 4:nb * 4 + 4], [f"pb{bg}"])
                        for kc in range(8):
                            mm(pbank[bu][:], wgu[:, kc, 1, fc * 128:(fc + 1) * 128], xT[:, kc, nb * 512:(nb + 1) * 512], kc == 0, kc == 7,
                               [wkey] + xT_r[nb * 4:nb * 4 + 4], [f"pb{bu}"])
                        fi = (f * 2 + nb) % 4
                        sg = ft[fi][:, 0:512]
                        act(sg, pbank[bg][:], AF.Silu, [f"pb{bg}"], [f"ft{fi}r"])
                        tt("dve", hT[:, f, nb * 512:(nb + 1) * 512], pbank[bu][:], sg, ALU.mult, [f"pb{bu}", f"ft{fi}r"], [f"A:hT{f}"])
            hkeys = [f"A:hT{f}" for f in range(22)]
            lnp = LNPipe()
            for cb in range(2):
                pieces = []
                for kg in range(3):
                    nk = 8 if kg < 2 else 6
                    wv, wk = wload(wd_d[kg * 1024:kg * 1024 + nk * 128, cb * 512:(cb + 1) * 512], nk, 512)
                    pieces.append((wv, wk))
                for t in range(8):
                    b = next_bank()
                    for f in range(22):
                        wv, wk = pieces[f // 8]
                        mm(pbank[b][:], hT[:, f, t * 128:(t + 1) * 128], wv[:, f % 8, :], f == 0, f == 21, [wk, hkeys[f]], [f"pb{b}"])
                    xs_ = xres[:, t, cb * 512:(cb + 1) * 512]
                    stt(xs_, xs_, ALPHA, pbank[b][:], ALU.mult, ALU.add, [f"xres{t}", f"pb{b}"], [f"xres{t}"])
                    if cb == 1:
                        lnp.push(t)
            lnp.flush()

        def ple(layer, hf, last):
            p16 = arena[:, 0:2048].rearrange("p (t n) -> p t n", t=8)
            pT = arena[:, 2048:4096].rearrange("p (k n) -> p k n", k=2)
            dma("pool", p16, D["p"][layer, hf * 1024:(hf + 1) * 1024, :].rearrange("(t p) n -> p t n", p=128), [], ["A:p16"])
            for t in range(8):
                pi = next_ptb()
                for k2 in range(2):
                    tr(ptb[pi][:, k2, :], p16[:, t, k2 * 128:(k2 + 1) * 128], ["A:p16"], [f"ptb{pi}"])
                cp("dve", pT[:, :, t * 128:(t + 1) * 128], ptb[pi][:, 0:2, :], [f"ptb{pi}"], ["A:pT"])
            wple, wplek = wload(D["w_ple"][layer], 2, 1024)
            wpg = [wload(D["w_pg"][layer][:, cb * 512:(cb + 1) * 512], 8, 512) for cb in range(2)]
            xtp = XTPipe()
            for t in range(8):
                for cb in range(2):
                    bg = next_bank()
                    bp = next_bank()
                    wv, wk = wpg[cb]
                    for kc in range(8):
                        mm(pbank[bg][:], xT[:, kc, t * 128:(t + 1) * 128], wv[:, kc, :], kc == 0, False, [wk, f"xT{t}"], [f"pb{bg}"])
                    mm(pbank[bg][:], ones_row[:], bpg_row[layer][:, cb * 512:(cb + 1) * 512], False, True, ["ones_row", "bpg"], [f"pb{bg}"])
                    for k2 in range(2):
                        mm(pbank[bp][:], pT[:, k2, t * 128:(t + 1) * 128], wple[:, k2, cb * 512:(cb + 1) * 512], k2 == 0, k2 == 1,
                           [wplek, "A:pT"], [f"pb{bp}"])
                    fi = (t * 2 + cb) % 4
                    gt = ft[fi][:, 0:512]
                    act(gt, pbank[bg][:], AF.Sigmoid, [f"pb{bg}"], [f"ft{fi}r"])
                    tt("dve", gt, pbank[bp][:], gt, ALU.mult, [f"pb{bp}", f"ft{fi}r"], [f"ft{fi}r"])
                    xs_ = xres[:, t, cb * 512:(cb + 1) * 512]
                    tt("dve", xs_, xs_, gt, ALU.add, [f"xres{t}", f"ft{fi}r"], [f"xres{t}"])
                if not last:
                    xtp.push(t)
            xtp.flush()

        def attn(hf):
            QT = arena[:, 0:8192].rearrange("p (k n) -> p k n", k=8)
            oT = QT
            KT = arena[:, 8192:8192 + 9216].rearrange("p (v k n) -> p v k n", v=2, k=4)
            V = arena[:, 17408:17408 + 2304].rearrange("p (t n) -> p t n", t=9)
            qk = lambda ch, t: f"A:Q{ch}_{t}"
            xT_r = [f"xT{t}" for t in range(8)]
            wq = D["w_qkv"]
            S.op("dve", lambda v: v.memset(KT[64:128, 0, :, :], 0.0), [], ["A:KTz0"])
            S.op("dve", lambda v: v.memset(KT[0:64, 1, :, :], 0.0), [], ["A:KTz1"])
            cp("dve", KT[0:64, 0, :, 0:128], kcarry[0:64, :, :], ["kcarry"], ["A:KTc0"])
            cp("dve", KT[64:128, 1, :, 0:128], kcarry[64:128, :, :], ["kcarry"], ["A:KTc1"])
            cp("dve", V[:, 0, :], vcarry[:], ["vcarry"], ["A:V0"])
            wkv, wkvk = wload(wq[:, 1024:1536], 8, 512)
            i = next_slot()
            kdup = slots[i][:, 0:4096].rearrange("p (k h r d) -> p k h r d", k=8, h=4, r=2)
            kdk = f"slot{i}"
            for r_ in range(2):
                S.op("dve", lambda v, r_=r_: v.tensor_copy(out=kdup[:, :, :, r_, :],
                                                        in_=wkv[:, :, 0:256].rearrange("p k (h d) -> p k h d", h=4)), [wkvk], [kdk])
            kdup2 = slots[i][:, 0:4096].rearrange("p (k h m) -> p k h m", k=8, h=4)
            for k in range(4):
                for nb in range(2):
                    b = next_bank()
                    for kc in range(8):
                        mm(pbank[b][:], kdup2[:, kc, k, :], xT[:, kc, nb * 512:(nb + 1) * 512], kc == 0, kc == 7,
                           [kdk] + xT_r[nb * 4:nb * 4 + 4], [f"pb{b}"])
                    csl = slice(128 + nb * 512, 128 + (nb + 1) * 512)
                    act(KT[0:64, 0, k, csl], pbank[b][0:64, :], AF.Identity, [f"pb{b}", "bk_dup"], ["A:KT0"], bias=bk_dup[0:64, k:k + 1])
                    act(KT[64:128, 1, k, csl], pbank[b][64:128, :], AF.Identity, [f"pb{b}", "bk_dup"], ["A:KT1"], bias=bk_dup[64:128, k:k + 1])
            for t in range(8):
                b = next_bank()
                for kc in range(8):
                    mm(pbank[b][:, 0:256], xT[:, kc, t * 128:(t + 1) * 128], wkv[:, kc, 256:512], kc == 0, False, [wkvk, f"xT{t}"], [f"pb{b}"])
                mm(pbank[b][:, 0:256], ones_row[:], brow[:, 0:256], False, True, ["ones_row", "brow"], [f"pb{b}"])
                cp("act", V[:, 1 + t, :], pbank[b][:, 0:256], [f"pb{b}"], [f"A:V{1 + t}"])
            for blk in range(2):
                wv, wk = wload(wq[:, blk * 512:(blk + 1) * 512], 8, 512)
                for jj in range(4):
                    j = blk * 4 + jj
                    for nb in range(2):
                        b = next_bank()
                        for kc in range(8):
                            mm(pbank[b][:], wv[:, kc, jj * 128:(jj + 1) * 128], xT[:, kc, nb * 512:(nb + 1) * 512], kc == 0, kc == 7,
                               [wk] + xT_r[nb * 4:nb * 4 + 4], [f"pb{b}"])
                        act(QT[:, j, nb * 512:(nb + 1) * 512], pbank[b][:], AF.Identity, [f"pb{b}", "bq_fm"],
                            [qk(j, nb * 4 + i_) for i_ in range(4)], bias=bq_fm[:, j:j + 1])
            cp("dve", kcarry[0:64, :, :], KT[0:64, 0, :, 1024:1152], ["A:KT0"], ["kcarry"])
            cp("dve", kcarry[64:128, :, :], KT[64:128, 1, :, 1024:1152], ["A:KT1"], ["kcarry"])
            cp("dve", vcarry[:], V[:, 8, :], ["A:V8"], ["vcarry"])
            ktkeys = ["A:KT0", "A:KT1", "A:KTz0", "A:KTz1", "A:KTc0", "A:KTc1"]
            def mi_of(qi):
                return 0 if (hf == 0 and qi == 0) else 1

            def at_A1(it):
                qi, k = it // 4, it % 4
                par = it % 2
                pa, pb_ = (0, 1) if par == 0 else (2, 3)
                for hh in range(4):
                    h = 4 * k + hh
                    ch, hv = h // 2, h % 2
                    bnk = pa if hh < 2 else pb_
                    o_ = pbank[bnk][:, (hh % 2) * 256:(hh % 2 + 1) * 256]
                    mm(o_, QT[:, ch, qi * 128:(qi + 1) * 128],
                       KT[:, hv, k, qi * 128:qi * 128 + 256], True, False, [qk(ch, qi)] + ktkeys, [f"pb{bnk}"])
                    mm(o_, ident_b[:], amask[:, mi_of(qi), :], False, True, ["ident_b", "amask"], [f"pb{bnk}"])

            def at_A2(it):
                qi, k = it // 4, it % 4
                par = it % 2
                pa, pb_ = (0, 1) if par == 0 else (2, 3)
                mx = sm8[:, 2 + par, 0:4]
                for half_ in range(2):
                    bnk = pa if half_ == 0 else pb_
                    mxh = sm8[:, 2 + par, half_ * 2:half_ * 2 + 2]
                    pv_ = pbank[bnk][:].rearrange("p (h s) -> p h s", h=2)
                    S.op("dve", lambda v, mxh=mxh, pv_=pv_: v.tensor_reduce(out=mxh, in_=pv_, axis=AX.X, op=ALU.max),
                         [f"pb{bnk}"], [f"sm_mx{par}"], cost=0.7)
                nmx = sm8[:, 2 + par, 4:8]
                stt(nmx, mx, -0.125, nsink[:, 4 * k:4 * k + 4], ALU.mult, ALU.min, [f"sm_mx{par}", "nsink"], [f"sm_nmx{par}"])
                pbf = ht[2 + par][:, 0:1024].rearrange("p (h s) -> p h s", h=4)
                ssum = sm8[:, 2 + par, 8:12]
                for hh in range(4):
                    bnk = pa if hh < 2 else pb_
                    act(pbf[:, hh, :], pbank[bnk][:, (hh % 2) * 256:(hh % 2 + 1) * 256], AF.Exp, [f"pb{bnk}", f"sm_nmx{par}"],
                        [f"ht{2 + par}", f"sm_ss{par}"], scale=0.125, bias=nmx[:, hh:hh + 1], accum_out=ssum[:, hh:hh + 1])
                es_ = sm8[:, 2 + par, 12:16]
                tt("dve", es_, srow[:, 3, 4 * k:4 * k + 4], nmx, ALU.add, ["srow", f"sm_nmx{par}"], [f"sm_es{par}"])
                act(es_, es_, AF.Exp, [f"sm_es{par}"], [f"sm_es{par}"])

            def at_A2b(it):
                qi, k = it // 4, it % 4
                par = it % 2
                ssum = sm8[:, 2 + par, 8:12]
                es_ = sm8[:, 2 + par, 12:16]
                tt("dve", es_, es_, ssum, ALU.add, [f"sm_es{par}", f"sm_ss{par}"], [f"sm_es{par}"])
                rv = sm8[:, 4 + (qi % 2), 4 * k:4 * k + 4]
                S.op("dve", lambda v, es_=es_, rv=rv: v.reciprocal(out=rv, in_=es_), [f"sm_es{par}"], [f"rinv{qi % 2}_{k}"])

            def at_T(it):
                qi, k = it // 4, it % 4
                par = it % 2
                pbf = ht[2 + par][:, 0:1024].rearrange("p (h s) -> p h s", h=4)
                pi = par
                for hh in range(4):
                    for kb in range(2):
                        tr(ptb[pi][:, hh * 2 + kb, :], pbf[:, hh, kb * 128:(kb + 1) * 128], [f"ht{2 + par}"], [f"ptb{pi}"])
                pts = ht[4 + par][:, 0:1024].rearrange("p (a q) -> p a q", a=8)
                cp("dve", pts, ptb[pi][:], [f"ptb{pi}"], [f"ht{4 + par}"])

            def at_PV(it):
                qi, k = it // 4, it % 4
                par = it % 2
                pts = ht[4 + par][:, 0:1024].rearrange("p (a q) -> p a q", a=8)
                for hh in range(4):
                    h = 4 * k + hh
                    for kb in range(2):
                        mm(pbank[4 + h // 8][:, (h % 8) * 64:(h % 8 + 1) * 64], pts[:, hh * 2 + kb, :], V[:, qi + kb, k * 64:(k + 1) * 64],
                           kb == 0, kb == 1, [f"ht{4 + par}", f"A:V{qi + kb}"], [f"pb{4 + h // 8}"])

            def at_tail_elem(qi):
                ob = ht[6 + qi % 2]
                obk = f"ht{6 + qi % 2}"
                for bb in range(2):
                    tt("dve", ob[:, bb * 512:(bb + 1) * 512].rearrange("p (h d) -> p h d", h=8),
                       pbank[4 + bb][:].rearrange("p (h d) -> p h d", h=8),
                       sm8[:, 4 + (qi % 2), bb * 8:(bb + 1) * 8].unsqueeze(2).to_broadcast([128, 8, 64]), ALU.mult,
                       [f"pb{4 + bb}"] + [f"rinv{qi % 2}_{k}" for k in range(4)], [obk])

            def at_tail_pe(qi):
                ob = ht[6 + qi % 2]
                obk = f"ht{6 + qi % 2}"
                pi = qi % 2
                for kc in range(8):
                    tr(ptb[pi][:, kc, :], ob[:, kc * 128:(kc + 1) * 128], [obk], [f"ptb{pi}"])
                cp("act", oT[:, :, qi * 128:(qi + 1) * 128], ptb[pi][:], [f"ptb{pi}"], [qk(ch_, qi) for ch_ in range(8)])

            NIT = 32
            pend = None
            for i in range(NIT + 2):
                if i < NIT:
                    at_A1(i)
                    at_A2(i)
                if 0 <= i - 1 < NIT:
                    at_A2b(i - 1)
                    at_T(i - 1)
                if 0 <= i - 2 < NIT:
                    at_PV(i - 2)
                    if pend is not None:
                        at_tail_pe(pend)
                        pend = None
                    if (i - 2) % 4 == 3:
                        at_tail_elem((i - 2) // 4)
                        pend = (i - 2) // 4
            at_tail_pe(pend)
            lnp = LNPipe()
            wo = [wload(D["w_o"][:, cb * 512:(cb + 1) * 512], 8, 512) for cb in range(2)]
            for t in range(8):
                for cb in range(2):
                    b = next_bank()
                    wv, wk = wo[cb]
                    for kc in range(8):
                        mm(pbank[b][:], oT[:, kc, t * 128:(t + 1) * 128], wv[:, kc, :], kc == 0, False, [wk, qk(kc, t)], [f"pb{b}"])
                    mm(pbank[b][:], ones_row[:], brow[:, 256 + cb * 512:256 + (cb + 1) * 512], False, True, ["ones_row", "brow"], [f"pb{b}"])
                    xs_ = xres[:, t, cb * 512:(cb + 1) * 512]
                    stt(xs_, xs_, ALPHA, pbank[b][:], ALU.mult, ALU.add, [f"xres{t}", f"pb{b}"], [f"xres{t}"])
                lnp.push(t)
            lnp.flush()

        bpg_t = sb("bpg_t", [128, 2048], BF16)
        bpg_row = [bpg_t[:, 0:1024], bpg_t[:, 1024:2048]]
        bpg_d = nc.dram_tensor("b_pg", [1, 2048], F32, kind="ExternalInput").ap()
        S.op("dve", lambda v: v.memset(bpg_t[:], 0.0), [], ["bpg"])
        dma("pool", bpg_t[0:1, :], bpg_d, [], ["bpg"])

        done = False
        for hf in range(2):
            if done:
                break
            for t in range(8):
                dma("sp", xres[:, t, :], D["x"][hf * 1024 + t * 128:hf * 1024 + (t + 1) * 128, :], [], [f"xres{t}"])
            xtp0 = XTPipe()
            for t in range(8):
                xtp0.push(t)
            xtp0.flush()
            for layer in range(2):
                load_row(0, D["ln_mix_g"][layer:layer + 1, :])
                load_row(1, D["ln_mix_b"][layer:layer + 1, :])
                if layer == 0:
                    load_row(2, D["normw"])
                    for sub in range(2):
                        mixer0(sub)
                    S.phase_barrier()
                else:
                    attn(hf)
                    S.phase_barrier()
                if stop_after == f"mix{layer}":
                    done = True
                    break
                load_row(0, D["ln_ffn_g"][layer:layer + 1, :])
                load_row(1, D["ln_ffn_b"][layer:layer + 1, :])
                ffn(layer)
                S.phase_barrier()
                if stop_after == f"ffn{layer}":
                    done = True
                    break
                ple(layer, hf, last=(layer == 1))
                S.phase_barrier()
                if stop_after == f"ple{layer}":
                    done = True
                    break
            for t in range(8):
                dma("sp", out_d[hf * 1024 + t * 128:hf * 1024 + (t + 1) * 128, :], xres[:, t, :], [f"xres{t}"], [f"out{hf}_{t}"])
        S.op("sp", lambda g: g.nop(), [k for k in S.lastw if k.startswith("out")], [])
        S.emit(sems)
        build.stats = S.stats
    return nc


def _consts():
    k = np.arange(128)
    ident = np.eye(128, dtype=np.float32)
    tri = (k[:, None] <= k[None, :]).astype(np.float32)
    negtri = -tri
    ones = np.ones((128, 128), np.float32)
    maskneg = np.where(k[None, :] < k[:, None], NEG, 0.0).astype(np.float32)
    am = np.full((128, 2, 256), NEG, np.float32)
    q = k[:, None]
    s = k[None, :]
    cur = np.where(s <= q, 0.0, NEG)
    prev = np.where(s > q, 0.0, NEG)
    am[:, 0, 128:] = cur
    am[:, 1, :128] = prev
    am[:, 1, 128:] = cur
    return dict(c_ident=ident, c_tri=tri, c_negtri=negtri, c_ones=ones, c_maskneg=maskneg, c_amask=(am * 8.0).astype(np.float32))


def _prep_shared(inp):
    f = lambda a: np.ascontiguousarray(np.asarray(a, dtype=np.float32))
    sh = {}
    sh["w_in"] = f(inp["w_in_mix"][0])
    cw = inp["ssd_conv_w"][0]
    sh["convw_fm"] = f(cw.T.reshape(12, 128, 4).transpose(1, 0, 2))
    sh["convb_fm"] = f(inp["ssd_conv_b"][0].reshape(12, 128).T)
    sh["dtb"] = f(inp["ssd_dt_bias"])
    sh["alog"] = f(inp["ssd_a_log"])
    sh["dsk"] = f(inp["ssd_d"])
    sh["normw"] = f(inp["ssd_norm_w"])
    sh["scw_fm"] = f(inp["sc_conv_w"][0].T.reshape(8, 128, 3).transpose(1, 0, 2))
    sh["w_out"] = f(inp["w_out_mix"][0])
    sh["w_qkv"] = f(inp["w_qkv"][0])
    bqkv = np.asarray(inp["b_qkv"][0], np.float32)
    sh["bq_fm"] = f(bqkv[:1024].reshape(8, 128).T)
    bk = bqkv[1024:1280].reshape(4, 64)
    sh["bk_dup"] = f(np.concatenate([bk, bk], axis=1).T)
    sh["brow"] = f(np.concatenate([bqkv[1280:1536], np.asarray(inp["b_o"][0], np.float32),
                                   np.zeros(1024, np.float32)])[None, :])
    sh["sinks"] = f(inp["attn_sinks"])
    sh["w_o"] = f(inp["w_o"][0])
    sh["ln_mix_g"] = f(inp["ln_mix_g"])
    sh["ln_mix_b"] = f(inp["ln_mix_b"])
    sh["w_g"] = f(inp["w_ffn_gate"])
    sh["w_u"] = f(inp["w_ffn_up"])
    sh["w_d"] = f(inp["w_ffn_down"])
    sh["ln_ffn_g"] = f(inp["ln_ffn_g"])
    sh["ln_ffn_b"] = f(inp["ln_ffn_b"])
    sh["w_ple"] = f(inp["w_ple"])
    sh["w_pg"] = f(inp["w_ple_gate"])
    sh["b_pg"] = f(np.asarray(inp["b_ple_gate"], np.float32).reshape(1, 2048))
    sh.update(_consts())
    return sh


_NC_CACHE = {}


def kernel(**inputs):
    stop_after = inputs.pop("_stop_after", None)
    cores = inputs.pop("_cores", 8)
    sh = _prep_shared(inputs)
    x = np.asarray(inputs["x"], np.float32)
    p = np.asarray(inputs["p"], np.float32)
    if stop_after not in _NC_CACHE:
        _NC_CACHE[stop_after] = build(stop_after)
    nc = _NC_CACHE[stop_after]
    in_maps = []
    for b in range(cores):
        m = dict(sh)
        m["x"] = np.ascontiguousarray(x[b])
        m["p"] = np.ascontiguousarray(p[:, b])
        in_maps.append(m)
    res = run_bass_kernel_spmd(nc, in_maps, core_ids=list(range(cores)))
    out = np.stack([np.asarray(r["out"], np.float32) for r in res.results], axis=0)
    return out
```

```python
from contextlib import ExitStack
import numpy as np
import concourse.bass as bass
import concourse.mybir as mybir
from concourse.bass_utils import run_bass_kernel_spmd

F32 = mybir.dt.float32
BF16 = mybir.dt.bfloat16
AF = mybir.ActivationFunctionType
ALU = mybir.AluOpType
AX = mybir.AxisListType

ENGS = ("pe", "act", "pool", "dve", "sp")
ALPHA = 4.0 ** 0.25
LN_EPS = 1e-5
RMS_EPS = 1e-5
NEG = -30000.0
DFF = 2816
NSLOT = 5
NDMA = 6


class Sched:
    def __init__(self, nc):
        self.nc = nc
        self.ops = []
        self.eng_ops = {e: [] for e in ENGS}
        self.lastw = {}
        self.readers = {}
        self.dma_rr = {e: 0 for e in ENGS}
        self.dma_last = {}
        self.arena_seen = set()

    def _add(self, eng, fn, reads, writes, kind, dom, cost=0.3, lat=None):
        reads = list(reads)
        writes = list(writes)
        if any(k.startswith("A:") for k in reads + writes):
            reads.append("PH")
            for k in reads + writes:
                if k.startswith("A:"):
                    self.arena_seen.add(k)
        idx = len(self.ops)
        deps = set()
        for r in reads:
            w = self.lastw.get(r)
            if w is not None:
                deps.add(w)
            if r.startswith("pb") or r.startswith("ptb"):
                for rd in self.readers.get(r, ()):
                    if self.ops[rd]["eng"] != eng:
                        deps.add(rd)
        for w_ in writes:
            w = self.lastw.get(w_)
            if w is not None:
                deps.add(w)
            for rd in self.readers.get(w_, ()):
                deps.add(rd)
        self.ops.append(dict(eng=eng, fn=fn, deps=deps, kind=kind, dom=dom, idx=idx, cost=cost, lat=lat))
        self.eng_ops[eng].append(idx)
        for r in reads:
            self.readers.setdefault(r, []).append(idx)
        for w_ in writes:
            self.lastw[w_] = idx
            self.readers[w_] = []
        return idx

    def op(self, eng, fn, reads=(), writes=(), cost=0.3):
        return self._add(eng, fn, reads, writes, "c", eng, cost=cost)

    def dma(self, eng, fn, reads=(), writes=(), nbytes=65536):
        slot = self.dma_rr[eng]
        self.dma_rr[eng] = (slot + 1) % NDMA
        dom = ("dma", eng, slot)
        idx = self._add(eng, fn, reads, writes, "d", dom, cost=(1.0 if eng == "pool" else 0.2), lat=nbytes)
        prev = self.dma_last.get(dom)
        if prev is not None:
            self.ops[idx]["deps"].add(prev)
        self.dma_last[dom] = idx
        return idx

    def phase_barrier(self):
        keys = sorted(self.arena_seen)
        self.arena_seen = set()
        idx = self._add("sp", lambda g: g.nop(), [], keys + ["PH"], "c", "sp", cost=0.05)
        return idx

    def schedule(self, window=48):
        import heapq
        ops = self.ops
        n = len(ops)
        succ = [[] for _ in range(n)]
        indeg = [0] * n
        for o in ops:
            indeg[o["idx"]] = len(o["deps"])
            for d in o["deps"]:
                succ[d].append(o["idx"])
        prio = [0.0] * n
        for i in range(n - 1, -1, -1):
            m = 0.0
            for s_ in succ[i]:
                if prio[s_] > m:
                    m = prio[s_]
            prio[i] = m + ops[i]["cost"]
        ready = {e: [] for e in ENGS}
        rtime = [0.0] * n
        for i in range(n):
            if indeg[i] == 0:
                heapq.heappush(ready[ops[i]["eng"]], i)
        etime = {e: 0.0 for e in ENGS}
        dma_free = 0.0
        order = {e: [] for e in ENGS}
        done = 0
        HOP = 0.15
        while done < n:
            best = None
            for e in ENGS:
                h = ready[e]
                if not h:
                    continue
                cand = heapq.nsmallest(window, h)
                for i in cand:
                    st = max(etime[e], rtime[i])
                    key = (int(st / 0.3), -prio[i], i)
                    if best is None or key < best[0]:
                        best = (key, e, i)
            key, e, i = best
            ready[e].remove(i)
            heapq.heapify(ready[e])
            st = max(etime[e], rtime[i])
            o = ops[i]
            etime[e] = st + o["cost"]
            if o["kind"] == "d":
                fin = max(dma_free, st + 1.5) + o["lat"] / 180e3
                dma_free = fin - 1.0
            else:
                fin = st + o["cost"]
            order[e].append(i)
            done += 1
            for s_ in succ[i]:
                if fin + HOP > rtime[s_]:
                    rtime[s_] = fin + HOP
                indeg[s_] -= 1
                if indeg[s_] == 0:
                    heapq.heappush(ready[ops[s_]["eng"]], s_)
        self.eng_ops = order
        self.est_makespan = max(etime.values())

    def emit(self, sems, reorder=True):
        ops = self.ops
        if reorder:
            self.schedule()
        pos = {}
        dom_seq = {}
        for e in ENGS:
            for i in self.eng_ops[e]:
                d = ops[i]["dom"]
                seq = dom_seq.setdefault(d, [])
                pos[i] = len(seq)
                seq.append(i)
        needed = set()
        waits = {}
        clock = {}
        known = {e: {} for e in ENGS}
        ptr = {e: 0 for e in ENGS}
        total = sum(len(v) for v in self.eng_ops.values())
        ndone = 0
        n_naive = 0
        while ndone < total:
            progressed = False
            for e in ENGS:
                lst = self.eng_ops[e]
                while ptr[e] < len(lst):
                    i = lst[ptr[e]]
                    op = ops[i]
                    if any(j not in clock for j in op["deps"]):
                        break
                    wl = {}
                    for j in op["deps"]:
                        d = ops[j]["dom"]
                        if d == e and op["kind"] == "c" and e == "pe":
                            continue
                        if d not in wl or pos[wl[d]] < pos[j]:
                            wl[d] = j
                    merged = known[e]
                    chosen = {}
                    copied = False
                    for d, j in sorted(wl.items(), key=lambda t: -t[1]):
                        if merged.get(d, -1) >= pos[j]:
                            continue
                        n_naive += 1
                        if not copied:
                            merged = dict(merged)
                            copied = True
                        chosen[d] = j
                        for dd, pp in clock[j].items():
                            if merged.get(dd, -1) < pp:
                                merged[dd] = pp
                    for d, j in chosen.items():
                        needed.add(j)
                    waits[i] = chosen
                    known[e] = merged
                    ck = dict(merged)
                    ck[op["dom"]] = pos[i]
                    clock[i] = ck
                    ptr[e] += 1
                    ndone += 1
                    progressed = True
            assert progressed, "schedule deadlock in wait derivation"
        count_at = {}
        cnt = {}
        for d, seq in dom_seq.items():
            c = 0
            for i in seq:
                if i in needed:
                    c += 1
                    count_at[i] = c
            cnt[d] = c
        self.stats = dict(n_ops=len(ops), n_signal=len(needed),
                          per_eng={e: len(v) for e, v in self.eng_ops.items()},
                          est_us=getattr(self, "est_makespan", None))
        engobj = {"pe": "tensor", "act": "scalar", "pool": "gpsimd", "dve": "vector", "sp": "sync"}

        def run_engine(e, eng):
            for i in self.eng_ops[e]:
                op = ops[i]
                for d, j in waits[i].items():
                    mult = 16 if isinstance(d, tuple) else 1
                    eng.wait_ge(sems[d], count_at[j] * mult)
                ins = op["fn"](eng)
                if i in needed:
                    d = op["dom"]
                    ins.then_inc(sems[d], 16 if isinstance(d, tuple) else 1)

        with self.nc.Block() as block:
            for e in ENGS:
                if not self.eng_ops[e]:
                    continue
                getattr(block, engobj[e])(lambda eng, e=e: run_engine(e, eng))


IN_SPECS = [
    ("x", [2048, 1024]), ("p", [2, 2048, 256]),
    ("w_in", [1024, 5648]), ("convw_fm", [128, 12, 4]), ("convb_fm", [128, 12]),
    ("dtb", [1, 16]), ("alog", [1, 16]), ("dsk", [1, 16]), ("normw", [1, 1024]),
    ("scw_fm", [128, 8, 3]), ("w_out", [2048, 1024]),
    ("w_qkv", [1024, 1536]), ("bq_fm", [128, 8]), ("bk_dup", [128, 4]), ("brow", [1, 2304]),
    ("sinks", [1, 16]), ("w_o", [1024, 1024]),
    ("ln_mix_g", [2, 1024]), ("ln_mix_b", [2, 1024]),
    ("w_g", [2, 1024, DFF]), ("w_u", [2, 1024, DFF]), ("w_d", [2, DFF, 1024]),
    ("ln_ffn_g", [2, 1024]), ("ln_ffn_b", [2, 1024]),
    ("w_ple", [2, 256, 1024]), ("w_pg", [2, 1024, 1024]),
    ("c_ident", [128, 128]), ("c_tri", [128, 128]), ("c_negtri", [128, 128]),
    ("c_ones", [128, 128]), ("c_maskneg", [128, 128]), ("c_amask", [128, 2, 256]),
]


def build(stop_after=None):
    nc = bass.Bass("TRN2", target_bir_lowering=False)
    D = {}
    for name, shape in IN_SPECS:
        D[name] = nc.dram_tensor(name, shape, F32, kind="ExternalInput").ap()
    out_d = nc.dram_tensor("out", [2048, 1024], F32, kind="ExternalOutput").ap()

    with ExitStack() as es:
        def sb(name, shape, dt):
            return es.enter_context(nc.sbuf_tensor("sb_" + name, shape, dt))

        def ps(name, shape, dt):
            return es.enter_context(nc.psum_tensor("ps_" + name, shape, dt))

        xres = sb("xres", [128, 8, 1024], F32)
        xT = sb("xT", [128, 8, 1024], BF16)
        slots = [sb(f"slot{i}", [128, 4224], BF16) for i in range(NSLOT)]
        ARENA_N = 23296
        arena = sb("arena", [128, ARENA_N], BF16)
        rows = [sb(f"row{i}", [128, 1024], F32) for i in range(3)]
        ft = [sb(f"ft{i}", [128, 1032], F32) for i in range(4)]
        ht = [sb(f"ht{i}", [128, 1024], BF16) for i in range(8)]
        st32 = sb("st32", [128, 2, 512], F32)
        st16 = sb("st16", [128, 2, 2, 512], BF16)
        brow = sb("brow", [128, 2304], BF16)
        ones_row = sb("ones_row", [128, 128], BF16)
        ident_f = sb("ident_f", [128, 128], F32)
        ident_b = sb("ident_b", [128, 128], BF16)
        tri_f = sb("tri_f", [128, 128], F32)
        negtri_f = sb("negtri_f", [128, 128], F32)
        ones_f = sb("ones_f", [128, 128], F32)
        maskneg_b = sb("maskneg_b", [128, 128], BF16)
        tri_b = sb("tri_b", [128, 128], BF16)
        negtri_b = sb("negtri_b", [128, 128], BF16)
        dhl = sb("dhl", [128, 2, 64], BF16)
        amask = sb("amask", [128, 2, 256], BF16)
        convw = sb("convw", [128, 12, 4], F32)
        convb = sb("convb", [128, 12], F32)
        scw = sb("scw", [128, 8, 3], F32)
        bq_fm = sb("bq_fm", [128, 8], F32)
        bk_dup = sb("bk_dup", [128, 4], F32)
        nsink = sb("nsink", [128, 16], F32)
        srow = sb("srow", [128, 4, 16], F32)
        halo_x = sb("halo_x", [128, 12, 3], F32)
        halo_s = sb("halo_s", [128, 8, 2], F32)
        kcarry = sb("kcarry", [128, 4, 128], BF16)
        vcarry = sb("vcarry", [128, 256], BF16)
        dts = sb("dts", [128, 12, 64], F32)
        stat = sb("stat", [128, 2, 6], F32)
        sm8 = sb("sm8", [128, 8, 16], F32)

        pbank = [ps(f"pb{i}", [128, 512], F32) for i in range(6)]
        ptb = [ps(f"ptb{i}", [128, 8, 128], BF16) for i in range(2)]

        sems = {}
        for e in ENGS:
            sems[e] = es.enter_context(nc.semaphore("s_" + e))
        for e in ("sp", "pool"):
            for k in range(NDMA):
                sems[("dma", e, k)] = es.enter_context(nc.semaphore(f"d_{e}_{k}"))
        S = Sched(nc)

        def nfree(ap):
            try:
                return int(ap.free_size())
            except Exception:
                return 512

        def mm(out, lhsT, rhs, start, stop, r, w):
            n_ = nfree(rhs)
            passes = 4 if lhsT.dtype == F32 else 1
            S.op("pe", lambda t: t.matmul(out, lhsT=lhsT, rhs=rhs, start=start, stop=stop), r, w,
                 cost=0.03 + passes * max(n_, 64) / 2000.0)

        def tr(out, in_, r, w):
            S.op("pe", lambda t: t.transpose(out=out, in_=in_, identity=ident_b[:]), list(r) + ["ident_b"], w, cost=0.09)

        def act(out, in_, func, r, w, **kw):
            S.op("act", lambda a: a.activation(out=out, in_=in_, func=func, **kw), r, w,
                 cost=0.22 + nfree(out) / 1050.0 + (0.12 if "accum_out" in kw else 0.0))

        def dcost(out):
            return 0.12 + nfree(out) / 900.0

        def tt(eng, out, in0, in1, op, r, w):
            S.op(eng, lambda v: v.tensor_tensor(out=out, in0=in0, in1=in1, op=op), r, w, cost=dcost(out))

        def ts(eng, out, in0, s1, s2, op0, op1, r, w):
            if s2 is None:
                S.op(eng, lambda v: v.tensor_scalar(out=out, in0=in0, scalar1=s1, scalar2=None, op0=op0), r, w, cost=dcost(out))
            else:
                S.op(eng, lambda v: v.tensor_scalar(out=out, in0=in0, scalar1=s1, scalar2=s2, op0=op0, op1=op1), r, w, cost=dcost(out))

        def stt(out, in0, scalar, in1, op0, op1, r, w):
            S.op("dve", lambda v: v.scalar_tensor_tensor(out=out, in0=in0, scalar=scalar, in1=in1, op0=op0, op1=op1), r, w, cost=dcost(out))

        def cp(eng, out, in_, r, w):
            if eng == "act":
                act(out, in_, AF.Copy, r, w)
            else:
                S.op(eng, lambda v: v.tensor_copy(out=out, in_=in_), r, w, cost=dcost(out))

        def dma(q, out, in_, r, w):
            try:
                nb = int(in_.partition_size()) * int(in_.free_size()) * 4
            except Exception:
                nb = 65536
            S.dma(q, lambda g: g.dma_start(out=out, in_=in_), r, w, nbytes=nb)

        slot_rr = [0]

        def next_slot():
            i = slot_rr[0]
            slot_rr[0] = (i + 1) % NSLOT
            return i

        def wload(dram_ap, kc, ncol):
            i = next_slot()
            view = slots[i][:, 0:kc * ncol].rearrange("p (k n) -> p k n", k=kc)
            dma("pool", view, dram_ap.rearrange("(k p) n -> p k n", p=128), [], [f"slot{i}"])
            return view, f"slot{i}"

        bank_rr = [0]

        def next_bank(n=4):
            i = bank_rr[0]
            bank_rr[0] = (i + 1) % n
            return i

        ptb_rr = [0]

        def next_ptb():
            i = ptb_rr[0]
            ptb_rr[0] = (i + 1) % 2
            return i

        xb_rr = [0]

        dma("sp", ident_f[:], D["c_ident"], [], ["ident_f"])
        dma("sp", tri_f[:], D["c_tri"], [], ["tri_f"])
        dma("sp", negtri_f[:], D["c_negtri"], [], ["negtri_f"])
        dma("sp", ones_f[:], D["c_ones"], [], ["ones_f"])
        dma("pool", amask[:], D["c_amask"], [], ["amask"])
        dma("pool", maskneg_b[:], D["c_maskneg"], [], ["maskneg_b"])
        S.op("dve", lambda v: v.memset(brow[:], 0.0), [], ["brow"])
        dma("pool", brow[0:1, :], D["brow"], [], ["brow"])
        dma("sp", convw[:], D["convw_fm"], [], ["convw"])
        dma("sp", convb[:], D["convb_fm"], [], ["convb"])
        dma("sp", scw[:], D["scw_fm"], [], ["scw"])
        dma("sp", bq_fm[:], D["bq_fm"], [], ["bq_fm"])
        dma("sp", bk_dup[:], D["bk_dup"], [], ["bk_dup"])
        for i, nm in enumerate(["dtb", "alog", "dsk", "sinks"]):
            dma("sp", srow[:, i, :], D[nm].partition_broadcast(128), [], ["srow"])
        cp("dve", ident_b[:], ident_f[:], ["ident_f"], ["ident_b"])
        cp("dve", tri_b[:], tri_f[:], ["tri_f"], ["tri_b"])
        cp("dve", negtri_b[:], negtri_f[:], ["negtri_f"], ["negtri_b"])
        S.op("dve", lambda v: v.memset(ones_row[:], 0.0), [], ["ones_row"])
        S.op("dve", lambda v: v.memset(ones_row[0:1, :], 1.0), ["ones_row"], ["ones_row"])
        S.op("dve", lambda v: v.memset(st32[:], 0.0), [], ["st32_0", "st32_1"])
        S.op("dve", lambda v: v.memset(st16[:], 0.0), [], ["st16_0_0", "st16_0_1", "st16_1_0", "st16_1_1"])
        S.op("dve", lambda v: v.memset(halo_x[:], 0.0), [], ["halo_x"])
        S.op("dve", lambda v: v.memset(halo_s[:], 0.0), [], ["halo_s"])
        S.op("dve", lambda v: v.memset(kcarry[:], 0.0), [], ["kcarry"])
        S.op("dve", lambda v: v.memset(vcarry[:], 0.0), [], ["vcarry"])
        ts("dve", nsink[:], srow[:, 3, :], -1.0, None, ALU.mult, None, ["srow"], ["nsink"])
        act(srow[:, 1, :], srow[:, 1, :], AF.Exp, ["srow"], ["srow"])
        ts("dve", srow[:, 1, :], srow[:, 1, :], -1.0, None, ALU.mult, None, ["srow"], ["srow"])

        def to_xT_a(tile):
            hb = xb_rr[0]
            xb_rr[0] = 1 - hb
            act(ht[hb][:], xres[:, tile, :], AF.Copy, [f"xres{tile}"], [f"ht{hb}"])
            return (tile, hb)

        def to_xT_b(tok):
            tile, hb = tok
            pi = next_ptb()
            for kc in range(8):
                tr(ptb[pi][:, kc, :], ht[hb][:, kc * 128:(kc + 1) * 128], [f"ht{hb}"], [f"ptb{pi}"])
            cp("dve", xT[:, :, tile * 128:(tile + 1) * 128], ptb[pi][:], [f"ptb{pi}"], [f"xT{tile}"])

        def to_xT(tile):
            to_xT_b(to_xT_a(tile))

        class XTPipe:
            def __init__(self):
                self.pend = None

            def push(self, tile):
                tok = to_xT_a(tile)
                if self.pend is not None:
                    to_xT_b(self.pend)
                self.pend = tok

            def flush(self):
                if self.pend is not None:
                    to_xT_b(self.pend)
                self.pend = None

        def ln_a(tile):
            k = f"xres{tile}"
            r = 6 + tile % 2
            q = f"_{tile % 2}"
            S.op("dve", lambda v: v.bn_stats(out=stat[:, 0, :], in_=xres[:, tile, 0:512]), [k], ["stat"])
            S.op("dve", lambda v: v.bn_stats(out=stat[:, 1, :], in_=xres[:, tile, 512:1024]), [k], ["stat"])
            S.op("dve", lambda v: v.bn_aggr(out=sm8[:, r, 0:2], in_=stat[:]), ["stat"], ["ln_mv" + q])
            ts("dve", sm8[:, r, 2:3], sm8[:, r, 1:2], LN_EPS, None, ALU.add, None, ["ln_mv" + q], ["ln_a" + q])
            act(sm8[:, r, 3:4], sm8[:, r, 2:3], AF.Sqrt, ["ln_a" + q], ["ln_b" + q])
            S.op("dve", lambda v: v.reciprocal(out=sm8[:, r, 4:5], in_=sm8[:, r, 3:4]), ["ln_b" + q], ["ln_rstd" + q])
            stt(sm8[:, r, 5:6], sm8[:, r, 0:1], -1.0, sm8[:, r, 4:5], ALU.mult, ALU.mult, ["ln_mv" + q, "ln_rstd" + q], ["ln_nmr" + q])

        def ln_b_act(tile):
            xt = xres[:, tile, :]
            k = f"xres{tile}"
            r = 6 + tile % 2
            q = f"_{tile % 2}"
            act(xt, xt, AF.Identity, [k, "ln_rstd" + q, "ln_nmr" + q], [k], scale=sm8[:, r, 4:5], bias=sm8[:, r, 5:6])

        def ln_b_dve(tile, grow, brow_):
            xt = xres[:, tile, :]
            k = f"xres{tile}"
            tt("dve", xt, xt, rows[grow][:], ALU.mult, [k, f"row{grow}"], [k])
            tt("dve", xt, xt, rows[brow_][:], ALU.add, [k, f"row{brow_}"], [k])

        def layer_norm(tile, grow, brow_):
            ln_a(tile)
            ln_b_act(tile)
            ln_b_dve(tile, grow, brow_)

        class LNPipe:
            def __init__(self):
                self.pend = None
                self.xtp = XTPipe()

            def push(self, tile):
                if self.pend is not None:
                    ln_b_act(self.pend)
                ln_a(tile)
                if self.pend is not None:
                    ln_b_dve(self.pend, 0, 1)
                    self.xtp.push(self.pend)
                self.pend = tile

            def flush(self):
                if self.pend is not None:
                    ln_b_act(self.pend)
                    ln_b_dve(self.pend, 0, 1)
                    self.xtp.push(self.pend)
                self.pend = None
                self.xtp.flush()

        def load_row(i, dram_row):
            dma("sp", rows[i][:], dram_row.partition_broadcast(128), [], [f"row{i}"])

        A_SZ, A_XS, A_BTM, A_BT, A_CT, A_YC = 0, 4096, 8192, 9216, 10240, 11264

        def mixer0(sub):
            t0 = sub * 4
            tok0 = t0 * 128
            w_in = D["w_in"]
            sz = arena[:, A_SZ:A_SZ + 4096].rearrange("p (t n) -> p t n", t=4)
            xs_tm = arena[:, A_XS:A_XS + 4096].rearrange("p (t n) -> p t n", t=4)
            b_tm = arena[:, A_BTM:A_BTM + 1024].rearrange("p (t n) -> p t n", t=4)
            bT = arena[:, A_BT:A_BT + 1024].rearrange("p (g n) -> p g n", g=2)
            cT = arena[:, A_CT:A_CT + 1024].rearrange("p (g n) -> p g n", g=2)
            ycT = arena[:, A_YC:A_YC + 8192].rearrange("p (k n) -> p k n", k=16)
            xT_r = [f"xT{t0 + i}" for i in range(4)]

            wblk = {}
            conv_bank = [0]

            def get_w(blk):
                if blk not in wblk:
                    if blk < 2:
                        wblk[blk] = wload(w_in[:, 1024 + blk * 512:1024 + (blk + 1) * 512], 8, 512)
                    else:
                        wblk[blk] = wload(w_in[:, 2048:2576], 8, 528)
                return wblk[blk]

            def conv_A(j):
                wview, wkey = get_w(j // 4)
                c0 = (j % 4) * 128
                b = conv_bank[0]
                conv_bank[0] = (b + 1) % 3
                for kc in range(8):
                    mm(pbank[b][:], wview[:, kc, c0:c0 + 128], xT[:, kc, tok0:tok0 + 512], kc == 0, kc == 7,
                       [wkey] + xT_r, [f"pb{b}"])
                fi = j % 2
                acc = ft[fi][:, 516:1028]
                cp("act", ft[fi][:, 3:515], pbank[b][:], [f"pb{b}"], [f"ft{fi}r"])
                act(acc, pbank[b][:], AF.Identity, [f"pb{b}", "convw"], [f"ft{fi}a"], scale=convw[:, j, 3:4])
                cp("dve", ft[fi][:, 0:3], halo_x[:, j, :], ["halo_x"], [f"ft{fi}r"])
                for k in range(3):
                    stt(acc, ft[fi][:, k:k + 512], convw[:, j, k:k + 1], acc, ALU.mult, ALU.add,
                        [f"ft{fi}r", f"ft{fi}a", "convw"], [f"ft{fi}a"])
                cp("dve", halo_x[:, j, :], ft[fi][:, 512:515], [f"ft{fi}r"], ["halo_x"])
                acck = f"ft{fi}a"
                if j < 8:
                    hi = 2 + (j % 2)
                    act(ht[hi][:, 0:512], acc, AF.Silu, [acck, "convb"], [f"ht{hi}"], bias=convb[:, j:j + 1])
                elif j < 10:
                    act(bT[:, j - 8, :], acc, AF.Silu, [acck, "convb"], ["A:bT"], bias=convb[:, j:j + 1])
                else:
                    act(cT[:, j - 10, :], acc, AF.Silu, [acck, "convb"], ["A:cT"], bias=convb[:, j:j + 1])

            def conv_B(j):
                if j < 8:
                    hi = 2 + (j % 2)
                    pi = next_ptb()
                    for t in range(4):
                        tr(ptb[pi][:, t, :], ht[hi][:, t * 128:(t + 1) * 128], [f"ht{hi}"], [f"ptb{pi}"])
                    cp("dve", xs_tm[:, :, j * 128:(j + 1) * 128], ptb[pi][:, 0:4, :], [f"ptb{pi}"], ["A:xs_tm"])
                elif j < 10:
                    g = j - 8
                    pi = next_ptb()
                    for t in range(4):
                        tr(ptb[pi][:, t, :], bT[:, g, t * 128:(t + 1) * 128], ["A:bT"], [f"ptb{pi}"])
                    cp("dve", b_tm[:, :, g * 128:(g + 1) * 128], ptb[pi][:, 0:4, :], [f"ptb{pi}"], ["A:b_tm"])

            wsc_cache = {}

            def sc_chunk(j):
                hblk, jj = j // 4, j % 4
                if hblk not in wsc_cache:
                    wsc_cache[hblk] = [wload(w_in[:, 2576 + si * 1024 + hblk * 512:2576 + si * 1024 + (hblk + 1) * 512], 8, 512)
                                       for si in range(3)]
                wsc = wsc_cache[hblk]
                bk = {2: 3, 1: 4, 0: 5}
                for si in (2, 1, 0):
                    b = bk[si]
                    wv, wk = wsc[si]
                    for kc in range(8):
                        mm(pbank[b][:], wv[:, kc, jj * 128:(jj + 1) * 128], xT[:, kc, tok0:tok0 + 512], kc == 0, kc == 7,
                           [wk] + xT_r, [f"pb{b}"])
                fi = 2 + (j % 2)
                hs = ft[fi][:, 516:1028]
                cp("act", hs, pbank[bk[2]][:], [f"pb{bk[2]}"], [f"ft{fi}a"])
                tt("dve", ft[fi][:, 2:514], pbank[bk[1]][:], hs, ALU.mult, [f"pb{bk[1]}", f"ft{fi}a"], [f"ft{fi}r"])
                cp("dve", ft[fi][:, 0:2], halo_s[:, j, :], ["halo_s"], [f"ft{fi}r"])
                act(hs, ft[fi][:, 2:514], AF.Identity, [f"ft{fi}r", "scw"], [f"ft{fi}a"], scale=scw[:, j, 2:3])
                for k in range(2):
                    stt(hs, ft[fi][:, k:k + 512], scw[:, j, k:k + 1], hs, ALU.mult, ALU.add, [f"ft{fi}r", f"ft{fi}a", "scw"], [f"ft{fi}a"])
                cp("dve", halo_s[:, j, :], ft[fi][:, 512:514], [f"ft{fi}r"], ["halo_s"])
                tt("dve", ycT[:, 8 + j, :], pbank[bk[0]][:], hs, ALU.mult, [f"pb{bk[0]}", f"ft{fi}a"], ["A:ycT_sc"])

            order = [8, 9, 10, 11, 0, 1, 2, 3, 4, 5, 6, 7]
            sc_next = [0]
            for n_, j in enumerate(order):
                conv_A(j)
                if n_ >= 1:
                    conv_B(order[n_ - 1])
                if n_ % 3 != 2:
                    sc_chunk(sc_next[0])
                    sc_next[0] += 1
                if j == 11:
                    wv, wk = get_w(2)
                    for t in range(4):
                        for kc in range(8):
                            mm(pbank[4][:, t * 16:(t + 1) * 16], xT[:, kc, (t0 + t) * 128:(t0 + t + 1) * 128], wv[:, kc, 512:528],
                               kc == 0, kc == 7, [wk, f"xT{t0 + t}"], ["pb4"])
                    d3 = lambda i: dts[:, i, :].rearrange("p (t h) -> p t h", t=4)
                    tt("dve", d3(0), pbank[4][:, 0:64].rearrange("p (t h) -> p t h", t=4),
                       srow[:, 0, :].unsqueeze(1).to_broadcast([128, 4, 16]), ALU.add, ["pb4", "srow"], ["dts0"])
                    stt(dts[:, 1, :], dts[:, 0, :], -1.0, dts[:, 0, :], ALU.mult, ALU.max, ["dts0"], ["dts1"])
                    act(dts[:, 2, :], dts[:, 1, :], AF.Exp, ["dts1"], ["dts2"], scale=-1.0)
                    act(dts[:, 3, :], dts[:, 2, :], AF.Ln, ["dts2"], ["dts3"], bias=1.0)
                    stt(dts[:, 4, :], dts[:, 0, :], 0.0, dts[:, 3, :], ALU.max, ALU.add, ["dts0", "dts3"], ["dts4"])
                    tt("dve", d3(5), d3(4), srow[:, 1, :].unsqueeze(1).to_broadcast([128, 4, 16]), ALU.mult, ["dts4", "srow"], ["dts5"])
                    cp("dve", dhl[:, 0, :], dts[:, 5, :], ["dts5"], ["dhl0"])
                    cp("dve", dts[:, 11, :], dhl[:, 0, :], ["dhl0"], ["dts11"])
                    tt("dve", dts[:, 11, :], dts[:, 5, :], dts[:, 11, :], ALU.subtract, ["dts5", "dts11"], ["dts11"])
                    cp("dve", dhl[:, 1, :], dts[:, 11, :], ["dts11"], ["dhl1"])
                    mm(pbank[4][:, 64:128], tri_f[:], dts[:, 5, :], True, True, ["tri_f", "dts5"], ["pb4"])
                    mm(pbank[4][:, 128:192], ones_f[:], dts[:, 5, :], True, True, ["ones_f", "dts5"], ["pb4"])
                    cp("act", dts[:, 6, :], pbank[4][:, 64:128], ["pb4"], ["dts6"])
                    act(dts[:, 7, :], pbank[4][:, 64:128], AF.Exp, ["pb4"], ["dts7"])
                    tt("dve", dts[:, 8, :], pbank[4][:, 128:192], dts[:, 6, :], ALU.subtract, ["pb4", "dts6"], ["dts8"])
                    act(dts[:, 8, :], dts[:, 8, :], AF.Exp, ["dts8"], ["dts8"])
                    act(dts[:, 9, :], pbank[4][:, 128:192], AF.Exp, ["pb4"], ["dts9"])
                    tt("dve", dts[:, 10, :], dts[:, 4, :], dts[:, 8, :], ALU.mult, ["dts4", "dts8"], ["dts10"])
            conv_B(order[-1])

            for cb in range(2):
                wv, wk = wload(w_in[:, cb * 512:(cb + 1) * 512], 8, 512)
                for t in range(4):
                    b = next_bank()
                    for kc in range(8):
                        mm(pbank[b][:], xT[:, kc, (t0 + t) * 128:(t0 + t + 1) * 128], wv[:, kc, :], kc == 0, kc == 7,
                           [wk, f"xT{t0 + t}"], [f"pb{b}"])
                    act(sz[:, t, cb * 512:(cb + 1) * 512], pbank[b][:], AF.Silu, [f"pb{b}"], ["A:sz"])

            ft2b = ft[2][:].bitcast(BF16)
            xdt_buf = [(ht[4][:], ["ht4"]), (ft2b[:, 0:1024], ["ft2r", "ft2a"])]
            xw_buf = [(ht[5][:], ["ht5"]), (ft2b[:, 1024:2048], ["ft2r", "ft2a"])]
            xsd_buf = [(ht[2][:], ["ht2"]), (ht[3][:], ["ht3"])]

            def ssd_pre(c):
                par = c % 2
                xs3 = xs_tm[:, c, :].rearrange("p (h d) -> p h d", h=16)
                for (buf, keys), row, rk in ((xdt_buf[par], 4, "dts4"), (xw_buf[par], 10, "dts10")):
                    tt("dve", buf.rearrange("p (h d) -> p h d", h=16), xs3,
                       dts[:, row, c * 16:(c + 1) * 16].unsqueeze(2).to_broadcast([128, 16, 64]), ALU.mult, ["A:xs_tm", rk], keys)
                buf, keys = xsd_buf[par]
                tt("dve", buf.rearrange("p (h d) -> p h d", h=16), xs3,
                   srow[:, 2, :].unsqueeze(2).to_broadcast([128, 16, 64]), ALU.mult, ["A:xs_tm", "srow"], keys)

            def ssd_S1(it):
                c, g, q = it // 4, (it // 2) % 2, it % 2
                csl = slice(c * 128, (c + 1) * 128)
                if q == 0:
                    r = (2 * c + g) % 4
                    mm(pbank[5][:, r * 128:(r + 1) * 128], bT[:, g, csl], cT[:, g, csl], True, True, ["A:bT", "A:cT"], ["pb5"])
                sb_ = it % 2
                k_ = f"pb{sb_}"
                col0 = c * 16 + g * 8 + q * 4
                o3 = pbank[sb_][:].rearrange("p (h l) -> p h l", h=4)
                mm(o3, ident_b[:], maskneg_b[:].unsqueeze(1).to_broadcast([128, 4, 128]), True, False, ["ident_b", "maskneg_b"], [k_])
                for v in range(2):
                    mm(o3, negtri_b[:], dhl[:, v, col0:col0 + 4].unsqueeze(2).to_broadcast([128, 4, 128]), False, False,
                       [f"dhl{v}", "negtri_b"], [k_])
                for h4 in range(4):
                    o = pbank[sb_][:, h4 * 128:(h4 + 1) * 128]
                    for v in range(2):
                        mm(o, dhl[:, v, col0 + h4:col0 + h4 + 1].to_broadcast([128, 128]), tri_b[:], False, (h4 == 3 and v == 1),
                           [f"dhl{v}", "tri_b"], [k_])

            def ssd_S2(it):
                c, g, q = it // 4, (it // 2) % 2, it % 2
                sb_ = it % 2
                r = (2 * c + g) % 4
                dsl = slice(sb_ * 512, (sb_ + 1) * 512)
                act(ht[6][:, dsl], pbank[sb_][:], AF.Exp, [f"pb{sb_}"], [f"ht6_{sb_}"])
                tt("dve", ht[7][:, dsl].rearrange("p (h l) -> p h l", h=4), ht[6][:, dsl].rearrange("p (h l) -> p h l", h=4),
                   pbank[5][:, r * 128:(r + 1) * 128].unsqueeze(1).to_broadcast([128, 4, 128]), ALU.mult,
                   [f"ht6_{sb_}", "pb5"], [f"ht7_{sb_}"])

            def ssd_S3(it):
                c, g, q = it // 4, (it // 2) % 2, it % 2
                par = c % 2
                sb_ = it % 2
                csl = slice(c * 128, (c + 1) * 128)
                xdt, xdtk = xdt_buf[par]
                if q == 0:
                    xsd, xsdk = xsd_buf[par]
                    mm(pbank[2][:], ident_b[:], xsd[:, g * 512:(g + 1) * 512], True, False, ["ident_b"] + xsdk, ["pb2"])
                for h4 in range(4):
                    h = q * 4 + h4
                    hh = g * 8 + h
                    mm(pbank[2][:, h * 64:(h + 1) * 64], ht[7][:, sb_ * 512 + h4 * 128:sb_ * 512 + (h4 + 1) * 128],
                       xdt[:, hh * 64:(hh + 1) * 64], False, (q == 1 and h4 == 3), [f"ht7_{sb_}"] + xdtk, ["pb2"])
                if q == 1:
                    xw, xwk = xw_buf[par]
                    mm(pbank[3][:], cT[:, g, csl], st16[:, par, g, :], True, True, ["A:cT", f"st16_{par}_{g}"], ["pb3"])
                    mm(pbank[4][:], b_tm[:, c, g * 128:(g + 1) * 128], xw[:, g * 512:(g + 1) * 512], True, True,
                       ["A:b_tm"] + xwk, ["pb4"])

            def ssd_S4(c, g):
                par = c % 2
                yb = ft[c % 2]
                yk = [f"ft{c % 2}r", f"ft{c % 2}a"]
                y3 = yb[:, g * 512:(g + 1) * 512].rearrange("p (h d) -> p h d", h=8)
                tt("dve", y3, pbank[3][:].rearrange("p (h d) -> p h d", h=8),
                   dts[:, 7, c * 16 + g * 8:c * 16 + g * 8 + 8].unsqueeze(2).to_broadcast([128, 8, 64]), ALU.mult,
                   ["pb3", "dts7"], yk)
                tt("dve", yb[:, g * 512:(g + 1) * 512], yb[:, g * 512:(g + 1) * 512], pbank[2][:], ALU.add, yk + ["pb2"], yk)
                s3 = st32[:, g, :].rearrange("p (h d) -> p h d", h=8)
                tt("dve", s3, s3, dts[:, 9, c * 16 + g * 8:c * 16 + g * 8 + 8].unsqueeze(2).to_broadcast([128, 8, 64]), ALU.mult,
                   [f"st32_{g}", "dts9"], [f"st32_{g}"])
                tt("dve", st32[:, g, :], st32[:, g, :], pbank[4][:], ALU.add, [f"st32_{g}", "pb4"], [f"st32_{g}"])
                cp("act", st16[:, 1 - par, g, :], st32[:, g, :], [f"st32_{g}"], [f"st16_{1 - par}_{g}"])

            def ssd_tail_elem(c):
                yb = ft[c % 2]
                yk = [f"ft{c % 2}r", f"ft{c % 2}a"]
                tt("dve", yb[:, 0:1024], yb[:, 0:1024], sz[:, c, :], ALU.mult, yk + ["A:sz"], yk)
                junk = ft[3][:, 516:1028]
                for g in range(2):
                    act(junk, yb[:, g * 512:(g + 1) * 512], AF.Square, yk, ["ft3a", f"rms_ss{g}"], accum_out=sm8[:, 1, g:g + 1])
                ts("dve", sm8[:, 1, 2:4], sm8[:, 1, 0:2], 1.0 / 512.0, RMS_EPS, ALU.mult, ALU.add, ["rms_ss0", "rms_ss1"], ["rms_a"])
                act(sm8[:, 1, 4:6], sm8[:, 1, 2:4], AF.Sqrt, ["rms_a"], ["rms_b"])
                S.op("dve", lambda v: v.reciprocal(out=sm8[:, 1, 6:8], in_=sm8[:, 1, 4:6]), ["rms_b"], ["rms_r"])
                hb = c % 2
                for g in range(2):
                    stt(ht[hb][:, g * 512:(g + 1) * 512], yb[:, g * 512:(g + 1) * 512], sm8[:, 1, 6 + g:7 + g],
                        rows[2][:, g * 512:(g + 1) * 512], ALU.mult, ALU.mult, yk + ["rms_r", "row2"], [f"ht{hb}"])

            def ssd_tail_pe(c):
                hb = c % 2
                pi = next_ptb()
                for kc in range(8):
                    tr(ptb[pi][:, kc, :], ht[hb][:, kc * 128:(kc + 1) * 128], [f"ht{hb}"], [f"ptb{pi}"])
                cp("act", ycT[:, 0:8, c * 128:(c + 1) * 128], ptb[pi][:], [f"ptb{pi}"], ["A:ycT_ssd"])

            NIT = 16
            ssd_pre(0)
            ssd_S1(0)
            pend_s4 = None
            tail_q = []
            for it in range(NIT):
                c, g, q = it // 4, (it // 2) % 2, it % 2
                if it + 1 < NIT:
                    if (it + 1) % 4 == 0:
                        ssd_pre((it + 1) // 4)
                    ssd_S1(it + 1)
                ssd_S2(it)
                if pend_s4 is not None:
                    pc, pg = pend_s4
                    ssd_S4(pc, pg)
                    if pg == 1:
                        ssd_tail_elem(pc)
                        tail_q.append((it + 1, pc))
                    pend_s4 = None
                ssd_S3(it)
                if q == 1:
                    pend_s4 = (c, g)
                while tail_q and tail_q[0][0] <= it:
                    ssd_tail_pe(tail_q.pop(0)[1])
            pc, pg = pend_s4
            ssd_S4(pc, pg)
            ssd_tail_elem(pc)
            for _, tc_ in tail_q:
                ssd_tail_pe(tc_)
            ssd_tail_pe(pc)

            lnp = LNPipe()
            w_out = D["w_out"]
            for cb in range(2):
                wviews = []
                for kh in range(2):
                    wv, wk = wload(w_out[kh * 1024:(kh + 1) * 1024, cb * 512:(cb + 1) * 512], 8, 512)
                    wviews.append((wv, wk))
                for t in range(4):
                    b = next_bank()
                    for kc in range(16):
                        wv, wk = wviews[kc // 8]
                        mm(pbank[b][:], ycT[:, kc, t * 128:(t + 1) * 128], wv[:, kc % 8, :], kc == 0, kc == 15,
                           [wk, "A:ycT_ssd", "A:ycT_sc"], [f"pb{b}"])
                    xs_ = xres[:, t0 + t, cb * 512:(cb + 1) * 512]
                    stt(xs_, xs_, ALPHA, pbank[b][:], ALU.mult, ALU.add, [f"xres{t0 + t}", f"pb{b}"], [f"xres{t0 + t}"])
                    if cb == 1:
                        lnp.push(t0 + t)
            lnp.flush()

        def ffn(layer):
            hT = arena[:, 0:22 * 1024].rearrange("p (f n) -> p f n", f=22)
            xT_r = [f"xT{t}" for t in range(8)]
            wg_d, wu_d, wd_d = D["w_g"][layer], D["w_u"][layer], D["w_d"][layer]
            for fb in range(11):
                si_ = next_slot()
                wgu = slots[si_][:, 0:4096].rearrange("p (k s n) -> p k s n", k=8, s=2)
                wkey = f"slot{si_}"
                dma("pool", wgu[:, :, 0, :], wg_d[:, fb * 256:(fb + 1) * 256].rearrange("(k p) n -> p k n", p=128), [], [wkey])
                dma("pool", wgu[:, :, 1, :], wu_d[:, fb * 256:(fb + 1) * 256].rearrange("(k p) n -> p k n", p=128), [], [wkey])
                for fc in range(2):
                    f = fb * 2 + fc
                    for nb in range(2):
                        bg = next_bank()
                        bu = next_bank()
                        for kc in range(8):
                            mm(pbank[bg][:], wgu[:, kc, 0, fc * 128:(fc + 1) * 128], xT[:, kc, nb * 512:(nb + 1) * 512], kc == 0, kc == 7,
                               [wkey] + xT_r[nb * 4:nb * 4 + 4], [f"pb{bg}"])
                        for kc in range(8):
                            mm(pbank[bu][:], wgu[:, kc, 1, fc * 128:(fc + 1) * 128], xT[:, kc, nb * 512:(nb + 1) * 512], kc == 0, kc == 7,
                               [wkey] + xT_r[nb * 4:nb * 4 + 4], [f"pb{bu}"])
                        fi = (f * 2 + nb) % 4
                        sg = ft[fi][:, 0:512]
                        act(sg, pbank[bg][:], AF.Silu, [f"pb{bg}"], [f"ft{fi}r"])
                        tt("dve", hT[:, f, nb * 512:(nb + 1) * 512], pbank[bu][:], sg, ALU.mult, [f"pb{bu}", f"ft{fi}r"], [f"A:hT{f}"])
            hkeys = [f"A:hT{f}" for f in range(22)]
            lnp = LNPipe()
            for cb in range(2):
                pieces = []
                for kg in range(3):
                    nk = 8 if kg < 2 else 6
                    wv, wk = wload(wd_d[kg * 1024:kg * 1024 + nk * 128, cb * 512:(cb + 1) * 512], nk, 512)
                    pieces.append((wv, wk))
                for t in range(8):
                    b = next_bank()
                    for f in range(22):
                        wv, wk = pieces[f // 8]
                        mm(pbank[b][:], hT[:, f, t * 128:(t + 1) * 128], wv[:, f % 8, :], f == 0, f == 21, [wk, hkeys[f]], [f"pb{b}"])
                    xs_ = xres[:, t, cb * 512:(cb + 1) * 512]
                    stt(xs_, xs_, ALPHA, pbank[b][:], ALU.mult, ALU.add, [f"xres{t}", f"pb{b}"], [f"xres{t}"])
                    if cb == 1:
                        lnp.push(t)
            lnp.flush()

        def ple(layer, hf, last):
            p16 = arena[:, 0:2048].rearrange("p (t n) -> p t n", t=8)
            pT = arena[:, 2048:4096].rearrange("p (k n) -> p k n", k=2)
            dma("pool", p16, D["p"][layer, hf * 1024:(hf + 1) * 1024, :].rearrange("(t p) n -> p t n", p=128), [], ["A:p16"])
            for t in range(8):
                pi = next_ptb()
                for k2 in range(2):
                    tr(ptb[pi][:, k2, :], p16[:, t, k2 * 128:(k2 + 1) * 128], ["A:p16"], [f"ptb{pi}"])
                cp("dve", pT[:, :, t * 128:(t + 1) * 128], ptb[pi][:, 0:2, :], [f"ptb{pi}"], ["A:pT"])
            wple, wplek = wload(D["w_ple"][layer], 2, 1024)
            wpg = [wload(D["w_pg"][layer][:, cb * 512:(cb + 1) * 512], 8, 512) for cb in range(2)]
            xtp = XTPipe()
            for t in range(8):
                for cb in range(2):
                    bg = next_bank()
                    bp = next_bank()
                    wv, wk = wpg[cb]
                    for kc in range(8):
                        mm(pbank[bg][:], xT[:, kc, t * 128:(t + 1) * 128], wv[:, kc, :], kc == 0, False, [wk, f"xT{t}"], [f"pb{bg}"])
                    mm(pbank[bg][:], ones_row[:], bpg_row[layer][:, cb * 512:(cb + 1) * 512], False, True, ["ones_row", "bpg"], [f"pb{bg}"])
                    for k2 in range(2):
                        mm(pbank[bp][:], pT[:, k2, t * 128:(t + 1) * 128], wple[:, k2, cb * 512:(cb + 1) * 512], k2 == 0, k2 == 1,
                           [wplek, "A:pT"], [f"pb{bp}"])
                    fi = (t * 2 + cb) % 4
                    gt = ft[fi][:, 0:512]
                    act(gt, pbank[bg][:], AF.Sigmoid, [f"pb{bg}"], [f"ft{fi}r"])
                    tt("dve", gt, pbank[bp][:], gt, ALU.mult, [f"pb{bp}", f"ft{fi}r"], [f"ft{fi}r"])
                    xs_ = xres[:, t, cb * 512:(cb + 1) * 512]
                    tt("dve", xs_, xs_, gt, ALU.add, [f"xres{t}", f"ft{fi}r"], [f"xres{t}"])
                if not last:
                    xtp.push(t)
            xtp.flush()

        def attn(hf):
            QT = arena[:, 0:8192].rearrange("p (k n) -> p k n", k=8)
            oT = QT
            KT = arena[:, 8192:8192 + 9216].rearrange("p (v k n) -> p v k n", v=2, k=4)
            V = arena[:, 17408:17408 + 2304].rearrange("p (t n) -> p t n", t=9)
            qk = lambda ch, t: f"A:Q{ch}_{t}"
            xT_r = [f"xT{t}" for t in range(8)]
            wq = D["w_qkv"]
            S.op("dve", lambda v: v.memset(KT[64:128, 0, :, :], 0.0), [], ["A:KTz0"])
            S.op("dve", lambda v: v.memset(KT[0:64, 1, :, :], 0.0), [], ["A:KTz1"])
            cp("dve", KT[0:64, 0, :, 0:128], kcarry[0:64, :, :], ["kcarry"], ["A:KTc0"])
            cp("dve", KT[64:128, 1, :, 0:128], kcarry[64:128, :, :], ["kcarry"], ["A:KTc1"])
            cp("dve", V[:, 0, :], vcarry[:], ["vcarry"], ["A:V0"])
            wkv, wkvk = wload(wq[:, 1024:1536], 8, 512)
            i = next_slot()
            kdup = slots[i][:, 0:4096].rearrange("p (k h r d) -> p k h r d", k=8, h=4, r=2)
            kdk = f"slot{i}"
            for r_ in range(2):
                S.op("dve", lambda v, r_=r_: v.tensor_copy(out=kdup[:, :, :, r_, :],
                                                        in_=wkv[:, :, 0:256].rearrange("p k (h d) -> p k h d", h=4)), [wkvk], [kdk])
            kdup2 = slots[i][:, 0:4096].rearrange("p (k h m) -> p k h m", k=8, h=4)
            for k in range(4):
                for nb in range(2):
                    b = next_bank()
                    for kc in range(8):
                        mm(pbank[b][:], kdup2[:, kc, k, :], xT[:, kc, nb * 512:(nb + 1) * 512], kc == 0, kc == 7,
                           [kdk] + xT_r[nb * 4:nb * 4 + 4], [f"pb{b}"])
                    csl = slice(128 + nb * 512, 128 + (nb + 1) * 512)
                    act(KT[0:64, 0, k, csl], pbank[b][0:64, :], AF.Identity, [f"pb{b}", "bk_dup"], ["A:KT0"], bias=bk_dup[0:64, k:k + 1])
                    act(KT[64:128, 1, k, csl], pbank[b][64:128, :], AF.Identity, [f"pb{b}", "bk_dup"], ["A:KT1"], bias=bk_dup[64:128, k:k + 1])
            for t in range(8):
                b = next_bank()
                for kc in range(8):
                    mm(pbank[b][:, 0:256], xT[:, kc, t * 128:(t + 1) * 128], wkv[:, kc, 256:512], kc == 0, False, [wkvk, f"xT{t}"], [f"pb{b}"])
                mm(pbank[b][:, 0:256], ones_row[:], brow[:, 0:256], False, True, ["ones_row", "brow"], [f"pb{b}"])
                cp("act", V[:, 1 + t, :], pbank[b][:, 0:256], [f"pb{b}"], [f"A:V{1 + t}"])
            for blk in range(2):
                wv, wk = wload(wq[:, blk * 512:(blk + 1) * 512], 8, 512)
                for jj in range(4):
                    j = blk * 4 + jj
                    for nb in range(2):
                        b = next_bank()
                        for kc in range(8):
                            mm(pbank[b][:], wv[:, kc, jj * 128:(jj + 1) * 128], xT[:, kc, nb * 512:(nb + 1) * 512], kc == 0, kc == 7,
                               [wk] + xT_r[nb * 4:nb * 4 + 4], [f"pb{b}"])
                        act(QT[:, j, nb * 512:(nb + 1) * 512], pbank[b][:], AF.Identity, [f"pb{b}", "bq_fm"],
                            [qk(j, nb * 4 + i_) for i_ in range(4)], bias=bq_fm[:, j:j + 1])
            cp("dve", kcarry[0:64, :, :], KT[0:64, 0, :, 1024:1152], ["A:KT0"], ["kcarry"])
            cp("dve", kcarry[64:128, :, :], KT[64:128, 1, :, 1024:1152], ["A:KT1"], ["kcarry"])
            cp("dve", vcarry[:], V[:, 8, :], ["A:V8"], ["vcarry"])
            ktkeys = ["A:KT0", "A:KT1", "A:KTz0", "A:KTz1", "A:KTc0", "A:KTc1"]
            def mi_of(qi):
                return 0 if (hf == 0 and qi == 0) else 1

            def at_A1(it):
                qi, k = it // 4, it % 4
                par = it % 2
                pa, pb_ = (0, 1) if par == 0 else (2, 3)
                for hh in range(4):
                    h = 4 * k + hh
                    ch, hv = h // 2, h % 2
                    bnk = pa if hh < 2 else pb_
                    o_ = pbank[bnk][:, (hh % 2) * 256:(hh % 2 + 1) * 256]
                    mm(o_, QT[:, ch, qi * 128:(qi + 1) * 128],
                       KT[:, hv, k, qi * 128:qi * 128 + 256], True, False, [qk(ch, qi)] + ktkeys, [f"pb{bnk}"])
                    mm(o_, ident_b[:], amask[:, mi_of(qi), :], False, True, ["ident_b", "amask"], [f"pb{bnk}"])

            def at_A2(it):
                qi, k = it // 4, it % 4
                par = it % 2
                pa, pb_ = (0, 1) if par == 0 else (2, 3)
                mx = sm8[:, 2 + par, 0:4]
                for half_ in range(2):
                    bnk = pa if half_ == 0 else pb_
                    mxh = sm8[:, 2 + par, half_ * 2:half_ * 2 + 2]
                    pv_ = pbank[bnk][:].rearrange("p (h s) -> p h s", h=2)
                    S.op("dve", lambda v, mxh=mxh, pv_=pv_: v.tensor_reduce(out=mxh, in_=pv_, axis=AX.X, op=ALU.max),
                         [f"pb{bnk}"], [f"sm_mx{par}"], cost=0.7)
                nmx = sm8[:, 2 + par, 4:8]
                stt(nmx, mx, -0.125, nsink[:, 4 * k:4 * k + 4], ALU.mult, ALU.min, [f"sm_mx{par}", "nsink"], [f"sm_nmx{par}"])
                pbf = ht[2 + par][:, 0:1024].rearrange("p (h s) -> p h s", h=4)
                ssum = sm8[:, 2 + par, 8:12]
                for hh in range(4):
                    bnk = pa if hh < 2 else pb_
                    act(pbf[:, hh, :], pbank[bnk][:, (hh % 2) * 256:(hh % 2 + 1) * 256], AF.Exp, [f"pb{bnk}", f"sm_nmx{par}"],
                        [f"ht{2 + par}", f"sm_ss{par}"], scale=0.125, bias=nmx[:, hh:hh + 1], accum_out=ssum[:, hh:hh + 1])
                es_ = sm8[:, 2 + par, 12:16]
                tt("dve", es_, srow[:, 3, 4 * k:4 * k + 4], nmx, ALU.add, ["srow", f"sm_nmx{par}"], [f"sm_es{par}"])
                act(es_, es_, AF.Exp, [f"sm_es{par}"], [f"sm_es{par}"])

            def at_A2b(it):
                qi, k = it // 4, it % 4
                par = it % 2
                ssum = sm8[:, 2 + par, 8:12]
                es_ = sm8[:, 2 + par, 12:16]
                tt("dve", es_, es_, ssum, ALU.add, [f"sm_es{par}", f"sm_ss{par}"], [f"sm_es{par}"])
                rv = sm8[:, 4 + (qi % 2), 4 * k:4 * k + 4]
                S.op("dve", lambda v, es_=es_, rv=rv: v.reciprocal(out=rv, in_=es_), [f"sm_es{par}"], [f"rinv{qi % 2}_{k}"])

            def at_T(it):
                qi, k = it // 4, it % 4
                par = it % 2
                pbf = ht[2 + par][:, 0:1024].rearrange("p (h s) -> p h s", h=4)
                pi = par
                for hh in range(4):
                    for kb in range(2):
                        tr(ptb[pi][:, hh * 2 + kb, :], pbf[:, hh, kb * 128:(kb + 1) * 128], [f"ht{2 + par}"], [f"ptb{pi}"])
                pts = ht[4 + par][:, 0:1024].rearrange("p (a q) -> p a q", a=8)
                cp("dve", pts, ptb[pi][:], [f"ptb{pi}"], [f"ht{4 + par}"])

            def at_PV(it):
                qi, k = it // 4, it % 4
                par = it % 2
                pts = ht[4 + par][:, 0:1024].rearrange("p (a q) -> p a q", a=8)
                for hh in range(4):
                    h = 4 * k + hh
                    for kb in range(2):
                        mm(pbank[4 + h // 8][:, (h % 8) * 64:(h % 8 + 1) * 64], pts[:, hh * 2 + kb, :], V[:, qi + kb, k * 64:(k + 1) * 64],
                           kb == 0, kb == 1, [f"ht{4 + par}", f"A:V{qi + kb}"], [f"pb{4 + h // 8}"])

            def at_tail_elem(qi):
                ob = ht[6 + qi % 2]
                obk = f"ht{6 + qi % 2}"
                for bb in range(2):
                    tt("dve", ob[:, bb * 512:(bb + 1) * 512].rearrange("p (h d) -> p h d", h=8),
                       pbank[4 + bb][:].rearrange("p (h d) -> p h d", h=8),
                       sm8[:, 4 + (qi % 2), bb * 8:(bb + 1) * 8].unsqueeze(2).to_broadcast([128, 8, 64]), ALU.mult,
                       [f"pb{4 + bb}"] + [f"rinv{qi % 2}_{k}" for k in range(4)], [obk])

            def at_tail_pe(qi):
                ob = ht[6 + qi % 2]
                obk = f"ht{6 + qi % 2}"
                pi = qi % 2
                for kc in range(8):
                    tr(ptb[pi][:, kc, :], ob[:, kc * 128:(kc + 1) * 128], [obk], [f"ptb{pi}"])
                cp("act", oT[:, :, qi * 128:(qi + 1) * 128], ptb[pi][:], [f"ptb{pi}"], [qk(ch_, qi) for ch_ in range(8)])

            NIT = 32
            pend = None
            for i in range(NIT + 2):
                if i < NIT:
                    at_A1(i)
                    at_A2(i)
                if 0 <= i - 1 < NIT:
                    at_A2b(i - 1)
                    at_T(i - 1)
                if 0 <= i - 2 < NIT:
                    at_PV(i - 2)
                    if pend is not None:
                        at_tail_pe(pend)
                        pend = None
                    if (i - 2) % 4 == 3:
                        at_tail_elem((i - 2) // 4)
                        pend = (i - 2) // 4
            at_tail_pe(pend)
            lnp = LNPipe()
            wo = [wload(D["w_o"][:, cb * 512:(cb + 1) * 512], 8, 512) for cb in range(2)]
            for t in range(8):
                for cb in range(2):
                    b = next_bank()
                    wv, wk = wo[cb]
                    for kc in range(8):
                        mm(pbank[b][:], oT[:, kc, t * 128:(t + 1) * 128], wv[:, kc, :], kc == 0, False, [wk, qk(kc, t)], [f"pb{b}"])
                    mm(pbank[b][:], ones_row[:], brow[:, 256 + cb * 512:256 + (cb + 1) * 512], False, True, ["ones_row", "brow"], [f"pb{b}"])
                    xs_ = xres[:, t, cb * 512:(cb + 1) * 512]
                    stt(xs_, xs_, ALPHA, pbank[b][:], ALU.mult, ALU.add, [f"xres{t}", f"pb{b}"], [f"xres{t}"])
                lnp.push(t)
            lnp.flush()

        bpg_t = sb("bpg_t", [128, 2048], BF16)
        bpg_row = [bpg_t[:, 0:1024], bpg_t[:, 1024:2048]]
        bpg_d = nc.dram_tensor("b_pg", [1, 2048], F32, kind="ExternalInput").ap()
        S.op("dve", lambda v: v.memset(bpg_t[:], 0.0), [], ["bpg"])
        dma("pool", bpg_t[0:1, :], bpg_d, [], ["bpg"])

        done = False
        for hf in range(2):
            if done:
                break
            for t in range(8):
                dma("sp", xres[:, t, :], D["x"][hf * 1024 + t * 128:hf * 1024 + (t + 1) * 128, :], [], [f"xres{t}"])
            xtp0 = XTPipe()
            for t in range(8):
                xtp0.push(t)
            xtp0.flush()
            for layer in range(2):
                load_row(0, D["ln_mix_g"][layer:layer + 1, :])
                load_row(1, D["ln_mix_b"][layer:layer + 1, :])
                if layer == 0:
                    load_row(2, D["normw"])
                    for sub in range(2):
                        mixer0(sub)
                    S.phase_barrier()
                else:
                    attn(hf)
                    S.phase_barrier()
                if stop_after == f"mix{layer}":
                    done = True
                    break
                load_row(0, D["ln_ffn_g"][layer:layer + 1, :])
                load_row(1, D["ln_ffn_b"][layer:layer + 1, :])
                ffn(layer)
                S.phase_barrier()
                if stop_after == f"ffn{layer}":
                    done = True
                    break
                ple(layer, hf, last=(layer == 1))
                S.phase_barrier()
                if stop_after == f"ple{layer}":
                    done = True
                    break
            for t in range(8):
                dma("sp", out_d[hf * 1024 + t * 128:hf * 1024 + (t + 1) * 128, :], xres[:, t, :], [f"xres{t}"], [f"out{hf}_{t}"])
        S.op("sp", lambda g: g.nop(), [k for k in S.lastw if k.startswith("out")], [])
        S.emit(sems)
        build.stats = S.stats
    return nc


def _consts():
    k = np.arange(128)
    ident = np.eye(128, dtype=np.float32)
    tri = (k[:, None] <= k[None, :]).astype(np.float32)
    negtri = -tri
    ones = np.ones((128, 128), np.float32)
    maskneg = np.where(k[None, :] < k[:, None], NEG, 0.0).astype(np.float32)
    am = np.full((128, 2, 256), NEG, np.float32)
    q = k[:, None]
    s = k[None, :]
    cur = np.where(s <= q, 0.0, NEG)
    prev = np.where(s > q, 0.0, NEG)
    am[:, 0, 128:] = cur
    am[:, 1, :128] = prev
    am[:, 1, 128:] = cur
    return dict(c_ident=ident, c_tri=tri, c_negtri=negtri, c_ones=ones, c_maskneg=maskneg, c_amask=(am * 8.0).astype(np.float32))


def _prep_shared(inp):
    f = lambda a: np.ascontiguousarray(np.asarray(a, dtype=np.float32))
    sh = {}
    sh["w_in"] = f(inp["w_in_mix"][0])
    cw = inp["ssd_conv_w"][0]
    sh["convw_fm"] = f(cw.T.reshape(12, 128, 4).transpose(1, 0, 2))
    sh["convb_fm"] = f(inp["ssd_conv_b"][0].reshape(12, 128).T)
    sh["dtb"] = f(inp["ssd_dt_bias"])
    sh["alog"] = f(inp["ssd_a_log"])
    sh["dsk"] = f(inp["ssd_d"])
    sh["normw"] = f(inp["ssd_norm_w"])
    sh["scw_fm"] = f(inp["sc_conv_w"][0].T.reshape(8, 128, 3).transpose(1, 0, 2))
    sh["w_out"] = f(inp["w_out_mix"][0])
    sh["w_qkv"] = f(inp["w_qkv"][0])
    bqkv = np.asarray(inp["b_qkv"][0], np.float32)
    sh["bq_fm"] = f(bqkv[:1024].reshape(8, 128).T)
    bk = bqkv[1024:1280].reshape(4, 64)
    sh["bk_dup"] = f(np.concatenate([bk, bk], axis=1).T)
    sh["brow"] = f(np.concatenate([bqkv[1280:1536], np.asarray(inp["b_o"][0], np.float32),
                                   np.zeros(1024, np.float32)])[None, :])
    sh["sinks"] = f(inp["attn_sinks"])
    sh["w_o"] = f(inp["w_o"][0])
    sh["ln_mix_g"] = f(inp["ln_mix_g"])
    sh["ln_mix_b"] = f(inp["ln_mix_b"])
    sh["w_g"] = f(inp["w_ffn_gate"])
    sh["w_u"] = f(inp["w_ffn_up"])
    sh["w_d"] = f(inp["w_ffn_down"])
    sh["ln_ffn_g"] = f(inp["ln_ffn_g"])
    sh["ln_ffn_b"] = f(inp["ln_ffn_b"])
    sh["w_ple"] = f(inp["w_ple"])
    sh["w_pg"] = f(inp["w_ple_gate"])
    sh["b_pg"] = f(np.asarray(inp["b_ple_gate"], np.float32).reshape(1, 2048))
    sh.update(_consts())
    return sh


_NC_CACHE = {}


def kernel(**inputs):
    stop_after = inputs.pop("_stop_after", None)
    cores = inputs.pop("_cores", 8)
    sh = _prep_shared(inputs)
    x = np.asarray(inputs["x"], np.float32)
    p = np.asarray(inputs["p"], np.float32)
    if stop_after not in _NC_CACHE:
        _NC_CACHE[stop_after] = build(stop_after)
    nc = _NC_CACHE[stop_after]
    in_maps = []
    for b in range(cores):
        m = dict(sh)
        m["x"] = np.ascontiguousarray(x[b])
        m["p"] = np.ascontiguousarray(p[:, b])
        in_maps.append(m)
    res = run_bass_kernel_spmd(nc, in_maps, core_ids=list(range(cores)))
    out = np.stack([np.asarray(r["out"], np.float32) for r in res.results], axis=0)
    return out
```
